# Optimizing a Trainium2 kernel written in Bass

```python
import jax, jax.numpy as jnp
from jax import lax
import numpy as np

D_MODEL = 1024
BATCH = 16
SEQ = 4096
DEPTH = 2

CTX_LEN = 256
GRID_W = 64
N_MOD = 9
D_FF = 2816
MIX_W = D_MODEL
GLA_HEADS = 4
GLA_DV = MIX_W // 2
GLA_DK = GLA_DV // 2
GLA_HEAD_V = GLA_DV // GLA_HEADS
GLA_HEAD_K = GLA_DK // GLA_HEADS
GLA_LOWRANK = 16
GATE_NORM = 16.0
GLA_CHUNK = 64
POOL_W = MIX_W // 4
POOL_WINDOWS = (2, 4, 8, 16)
POOL_GROUP = POOL_W // len(POOL_WINDOWS)
CONV_W = MIX_W // 4
CONV_K = 31
EPS = 1e-6
SPLIT_SIZES = (GLA_DK, GLA_DK, GLA_DV, GLA_DV, GLA_LOWRANK, GLA_LOWRANK, POOL_W, 2 * CONV_W)
D_IN = sum(SPLIT_SIZES)
SPLIT_IDX = [int(i) for i in np.cumsum(SPLIT_SIZES)[:-1]]

kernel_name = "hybrid_gla_pool_conv_macaron_dit"


def rms_norm(x, g):
    xf = x.astype(jnp.float32)
    y = xf * lax.rsqrt(jnp.mean(xf * xf, axis=-1, keepdims=True) + EPS)
    return (y * g.astype(jnp.float32)).astype(x.dtype)


def layer_norm(x, g, b):
    xf = x.astype(jnp.float32)
    mu = jnp.mean(xf, axis=-1, keepdims=True)
    var = jnp.mean(jnp.square(xf - mu), axis=-1, keepdims=True)
    y = (xf - mu) * lax.rsqrt(var + EPS) * g.astype(jnp.float32) + b.astype(jnp.float32)
    return y.astype(x.dtype)


def swiglu_ffn(h, w_up, w_down):
    a, b = jnp.split(h @ w_up, 2, axis=-1)
    return (jax.nn.silu(a) * b) @ w_down


def sub_in(h, gains, mods, i):
    return rms_norm(h, gains[2 * i]) * (1.0 + mods[3 * i + 1]) + mods[3 * i]


def sub_out(h, y, gains, mods, i, weight):
    return h + weight * mods[3 * i + 2] * rms_norm(y, gains[2 * i + 1])


def box_mean(x, axis, w):
    n = x.shape[axis]
    lo = w // 2
    hi = w - 1 - lo
    cs = jnp.cumsum(x.astype(jnp.float32), axis=axis)
    pad = [(0, 0)] * x.ndim
    pad[axis] = (1, 0)
    cs = jnp.pad(cs, pad)
    idx = jnp.arange(n)
    top = jnp.minimum(idx + hi + 1, n)
    bot = jnp.maximum(idx - lo, 0)
    s = jnp.take(cs, top, axis=axis) - jnp.take(cs, bot, axis=axis)
    shape = [1] * x.ndim
    shape[axis] = n
    cnt = (top - bot).astype(jnp.float32).reshape(shape)
    return (s / cnt).astype(x.dtype)


def pool_mixer(u, pool_w, pool_scale, grid):
    B, N, _ = u.shape
    outs = []
    for gi, w in enumerate(POOL_WINDOWS):
        ug = u[..., gi * POOL_GROUP:(gi + 1) * POOL_GROUP]
        if grid:
            rows = N // GRID_W
            u2 = ug.reshape(B, rows, GRID_W, POOL_GROUP)
            m = box_mean(box_mean(u2, 1, w), 2, w).reshape(B, N, POOL_GROUP)
        else:
            m = box_mean(ug, 1, w)
        outs.append((m - ug) @ pool_w[gi])
    return jnp.concatenate(outs, axis=-1) * pool_scale


def conv_module(u, dw, dw_b, ln_g, ln_b, pw, pw_b):
    a, gt = jnp.split(u, 2, axis=-1)
    h = a * jax.nn.sigmoid(gt)
    h = lax.conv_general_dilated(h, dw[:, None, :], window_strides=(1,),
                                 padding=[(CONV_K // 2, CONV_K // 2)],
                                 dimension_numbers=('NWC', 'WIO', 'NWC'),
                                 feature_group_count=CONV_W) + dw_b
    h = jax.nn.silu(layer_norm(h, ln_g, ln_b))
    return h @ pw + pw_b


def gla_chunk_scan(q, k, v, logg, s0):
    B, N, H, _ = q.shape
    dv = v.shape[-1]
    nc = N // GLA_CHUNK

    def to_chunks(t):
        return t.astype(jnp.float32).reshape(B, nc, GLA_CHUNK, H, t.shape[-1]).transpose(1, 0, 3, 2, 4)

    lower = jnp.tril(jnp.ones((GLA_CHUNK, GLA_CHUNK), dtype=bool))[:, :, None]

    def step(s, inp):
        qc, kc, vc, gc = inp
        b = jnp.cumsum(gc, axis=2)
        o_inter = jnp.einsum('bhik,bhkv->bhiv', qc * jnp.exp(b), s)
        diff = b[:, :, :, None, :] - b[:, :, None, :, :]
        decay = jnp.exp(jnp.where(lower, diff, -jnp.inf))
        att = jnp.einsum('bhijk,bhjk->bhij', qc[:, :, :, None, :] * decay, kc)
        o_intra = jnp.einsum('bhij,bhjv->bhiv', att, vc)
        b_last = b[:, :, -1:, :]
        s_new = jnp.exp(b_last[:, :, 0, :, None]) * s + jnp.einsum(
            'bhjk,bhjv->bhkv', kc * jnp.exp(b_last - b), vc)
        return s_new, o_inter + o_intra

    s_fin, o = lax.scan(step, s0, (to_chunks(q), to_chunks(k), to_chunks(v), to_chunks(logg)))
    o = o.transpose(1, 0, 3, 2, 4).reshape(B, N, H, dv)
    return o, s_fin


def gla_bidir(q, k, v, g_f, g_b, s_f, s_b):
    o_f, s_f_fin = gla_chunk_scan(q, k, v, g_f, s_f)
    flip = lambda t: jnp.flip(t, axis=1)
    o_b, s_b_fin = gla_chunk_scan(flip(q), flip(k), flip(v), flip(g_b), s_b)
    return o_f + flip(o_b), s_f_fin, s_b_fin


def gla_inputs(parts, w_gk2, b_gk):
    q, k, v, _, lr_f, lr_b = parts[:6]
    B, N, _ = q.shape
    heads_k = lambda t: t.reshape(B, N, GLA_HEADS, GLA_HEAD_K)

    def log_decay(lr, d):
        logit = (lr @ w_gk2[d] + b_gk[d]).astype(jnp.float32)
        return heads_k(jax.nn.log_sigmoid(logit) / GATE_NORM)

    return (heads_k(q) * GLA_HEAD_K ** -0.5, heads_k(k),
            v.reshape(B, N, GLA_HEADS, GLA_HEAD_V), log_decay(lr_f, 0), log_decay(lr_b, 1))


def mixer_output(parts, o_gla, grid, gla_norm_g, pool_w, pool_scale, conv_dw, conv_dw_b,
                 conv_ln_g, conv_ln_b, conv_pw, conv_pw_b, w_out):
    g = parts[3]
    B, N = g.shape[:2]
    y_gla = rms_norm(o_gla, gla_norm_g).reshape(B, N, GLA_DV).astype(g.dtype) * jax.nn.silu(g)
    y_pool = pool_mixer(parts[6], pool_w, pool_scale, grid)
    y_conv = conv_module(parts[7], conv_dw, conv_dw_b, conv_ln_g, conv_ln_b, conv_pw, conv_pw_b)
    return jnp.concatenate([y_gla, y_pool, y_conv], axis=-1) @ w_out


def token_mix(h_lat, h_ctx, w_in, w_gk2, b_gk, gla_norm_g, pool_w, pool_scale, conv_dw, conv_dw_b,
              conv_ln_g, conv_ln_b, conv_pw, conv_pw_b, w_out, need_ctx):
    parts_ctx = jnp.split(h_ctx @ w_in, SPLIT_IDX, axis=-1)
    parts_lat = jnp.split(h_lat @ w_in, SPLIT_IDX, axis=-1)
    B = h_ctx.shape[0]
    zero = jnp.zeros((B, GLA_HEADS, GLA_HEAD_K, GLA_HEAD_V), jnp.float32)
    o_ctx, s_f, s_b = gla_bidir(*gla_inputs(parts_ctx, w_gk2, b_gk), zero, zero)
    o_lat, _, _ = gla_bidir(*gla_inputs(parts_lat, w_gk2, b_gk), s_f, s_b)
    tail = (gla_norm_g, pool_w, pool_scale, conv_dw, conv_dw_b, conv_ln_g, conv_ln_b, conv_pw, conv_pw_b, w_out)
    y_lat = mixer_output(parts_lat, o_lat, True, *tail)
    y_ctx = mixer_output(parts_ctx, o_ctx, False, *tail) if need_ctx else None
    return y_lat, y_ctx


def setup_inputs(seed: int = 0) -> dict:
    key = jax.random.key(seed)
    ks = jax.random.split(key, 26)
    D = D_MODEL
    nrm = lambda k, shape, scale: jax.random.normal(k, shape, jnp.float32) * scale
    return {
        "x": nrm(ks[0], (BATCH, SEQ, D), 1.0),
        "c": nrm(ks[1], (BATCH, D), 1.0),
        "ctx": nrm(ks[2], (BATCH, CTX_LEN, D), 1.0),
        "c_ctx": nrm(ks[3], (D,), 1.0),
        "w_ada": nrm(ks[4], (DEPTH, D, N_MOD * D), 0.5 * D ** -0.5),
        "b_ada": nrm(ks[5], (DEPTH, N_MOD * D), 0.02),
        "norm_g": 1.0 + nrm(ks[6], (DEPTH, 6, D), 0.05),
        "ffn1_up": nrm(ks[7], (DEPTH, D, 2 * D_FF), D ** -0.5),
        "ffn1_down": nrm(ks[8], (DEPTH, D_FF, D), D_FF ** -0.5),
        "ffn2_up": nrm(ks[9], (DEPTH, D, 2 * D_FF), D ** -0.5),
        "ffn2_down": nrm(ks[10], (DEPTH, D_FF, D), D_FF ** -0.5),
        "w_in": nrm(ks[11], (DEPTH, D, D_IN), D ** -0.5),
        "w_gk2": nrm(ks[12], (DEPTH, 2, GLA_LOWRANK, GLA_DK), GLA_LOWRANK ** -0.5),
        "b_gk": nrm(ks[13], (DEPTH, 2, GLA_DK), 0.1),
        "gla_norm_g": 1.0 + nrm(ks[14], (DEPTH, GLA_HEAD_V), 0.05),
        "pool_w": nrm(ks[15], (DEPTH, len(POOL_WINDOWS), POOL_GROUP, POOL_GROUP), POOL_GROUP ** -0.5),
        "pool_scale": 1.0 + nrm(ks[16], (DEPTH, POOL_W), 0.05),
        "conv_dw": nrm(ks[17], (DEPTH, CONV_K, CONV_W), CONV_K ** -0.5),
        "conv_dw_b": nrm(ks[18], (DEPTH, CONV_W), 0.02),
        "conv_ln_g": 1.0 + nrm(ks[19], (DEPTH, CONV_W), 0.05),
        "conv_ln_b": nrm(ks[20], (DEPTH, CONV_W), 0.02),
        "conv_pw": nrm(ks[21], (DEPTH, CONV_W, CONV_W), CONV_W ** -0.5),
        "conv_pw_b": nrm(ks[22], (DEPTH, CONV_W), 0.02),
        "w_out": nrm(ks[23], (DEPTH, MIX_W, D), MIX_W ** -0.5),
    }


def reference(x, c, ctx, c_ctx, w_ada, b_ada, norm_g, ffn1_up, ffn1_down, ffn2_up, ffn2_down,
              w_in, w_gk2, b_gk, gla_norm_g, pool_w, pool_scale, conv_dw, conv_dw_b,
              conv_ln_g, conv_ln_b, conv_pw, conv_pw_b, w_out):
    for l in range(DEPTH):
        last = l == DEPTH - 1
        ml = jnp.split((jax.nn.silu(c) @ w_ada[l] + b_ada[l])[:, None, :], N_MOD, axis=-1)
        mc = jnp.split((jax.nn.silu(c_ctx) @ w_ada[l] + b_ada[l])[None, None, :], N_MOD, axis=-1)
        gains = norm_g[l]
        x = sub_out(x, swiglu_ffn(sub_in(x, gains, ml, 0), ffn1_up[l], ffn1_down[l]), gains, ml, 0, 0.5)
        ctx = sub_out(ctx, swiglu_ffn(sub_in(ctx, gains, mc, 0), ffn1_up[l], ffn1_down[l]), gains, mc, 0, 0.5)
        y_lat, y_ctx = token_mix(sub_in(x, gains, ml, 1), sub_in(ctx, gains, mc, 1),
                                 w_in[l], w_gk2[l], b_gk[l], gla_norm_g[l], pool_w[l], pool_scale[l],
                                 conv_dw[l], conv_dw_b[l], conv_ln_g[l], conv_ln_b[l], conv_pw[l],
                                 conv_pw_b[l], w_out[l], not last)
        x = sub_out(x, y_lat, gains, ml, 1, 1.0)
        x = sub_out(x, swiglu_ffn(sub_in(x, gains, ml, 2), ffn2_up[l], ffn2_down[l]), gains, ml, 2, 0.5)
        if not last:
            ctx = sub_out(ctx, y_ctx, gains, mc, 1, 1.0)
            ctx = sub_out(ctx, swiglu_ffn(sub_in(ctx, gains, mc, 2), ffn2_up[l], ffn2_down[l]), gains, mc, 2, 0.5)
    return x
```

```python
import numpy as np
import concourse.bass as bass
import concourse.mybir as mybir
from concourse.bass_utils import run_bass_kernel_spmd

F32 = mybir.dt.float32
BF16 = mybir.dt.bfloat16
AF = mybir.ActivationFunctionType
ALU = mybir.AluOpType

ENGINES = ("tensor", "vector", "scalar", "gpsimd", "sync")
SEM_LIMIT = 2000


class V:
    __slots__ = ("ap", "key", "box")

    def __init__(self, ap, key, box):
        self.ap, self.key, self.box = ap, key, box

    def w(self, ap):
        return V(ap, self.key, self.box)


class TT:
    def __init__(self, name, handle, shape, is_dram=False):
        self.name, self.shape = name, tuple(int(s) for s in shape)
        self.base = handle.ap() if is_dram else handle[:]

    def __getitem__(self, idx):
        if not isinstance(idx, tuple):
            idx = (idx,)
        idx = tuple(idx) + (slice(None),) * (len(self.shape) - len(idx))
        box = []
        for i, n in zip(idx, self.shape):
            if isinstance(i, slice):
                lo = 0 if i.start is None else i.start
                hi = n if i.stop is None else i.stop
                assert i.step is None and 0 <= lo < hi <= n, (self.name, idx, self.shape)
                box.append((lo, hi))
            else:
                assert 0 <= i < n, (self.name, idx, self.shape)
                box.append((i, i + 1))
        return V(self.base[idx], self.name, tuple(box))


def _overlap(a, b):
    for (l0, h0), (l1, h1) in zip(a, b):
        if h0 <= l1 or h1 <= l0:
            return False
    return True


def _covers(a, b):
    for (l0, h0), (l1, h1) in zip(a, b):
        if l0 > l1 or h0 < h1:
            return False
    return True


class Op:
    __slots__ = ("eng", "fn", "deps", "signal", "dma_key", "ev", "waits", "id", "alld", "cost", "tag", "prio")


class Prog:
    def __init__(self, nc):
        self.nc = nc
        self.ops = []
        self.wr = {}
        self.rd = {}
        self.last_dma = {}
        self.psum_full = {}
        self.tensors = {}
        self._ctx = []

    def sbuf(self, name, shape, dtype):
        h = self.nc.alloc_sbuf_tensor(name, list(shape), dtype)
        t = TT(name, h, shape)
        self.tensors[name] = t
        return t

    def psum(self, name, shape, dtype):
        h = self.nc.alloc_psum_tensor(name, list(shape), dtype)
        t = TT(name, h, shape)
        self.tensors[name] = t
        self.psum_full[name] = tuple((0, int(n)) for n in shape)
        return t

    def dram(self, name, shape, dtype, kind="Internal"):
        h = self.nc.dram_tensor(name, list(shape), dtype, kind=kind)
        t = TT(name, h, shape, is_dram=True)
        self.tensors[name] = t
        return t

    def op(self, eng, fn, reads=(), writes=(), dma_key=None):
        o = Op()
        o.eng, o.fn, o.signal, o.dma_key, o.ev, o.waits = eng, fn, dma_key is not None, dma_key, None, None
        o.id = len(self.ops)
        o.tag = getattr(self, "cur_tag", "")
        o.prio = getattr(self, "cur_prio", 0)
        deps = {}
        if any(v.key in self.psum_full for v in reads) or any(v.key in self.psum_full for v in writes):
            reads = [V(v.ap, v.key, self.psum_full[v.key]) if v.key in self.psum_full else v for v in reads]
            writes = [V(v.ap, v.key, self.psum_full[v.key]) if v.key in self.psum_full else v for v in writes]
            writes = writes + [v for v in reads if v.key in self.psum_full]
        for r in reads:
            for box, pid in self.wr.get(r.key, ()):
                if _overlap(box, r.box):
                    deps[pid] = True
        for w in writes:
            for box, pid in self.wr.get(w.key, ()):
                if _overlap(box, w.box):
                    deps.setdefault(pid, False)
            for (box, _e), pid in self.rd.get(w.key, {}).items():
                if _overlap(box, w.box):
                    deps.setdefault(pid, False)
        if dma_key is not None and dma_key in self.last_dma:
            deps[self.last_dma[dma_key]] = True
        if dma_key is not None:
            self.last_dma[dma_key] = o.id
        deps.pop(o.id, None)
        o.alld = list(deps.keys())
        o.cost = self._cost(eng, reads, writes, dma_key)
        final = []
        for pid, raw in deps.items():
            p = self.ops[pid]
            if p.dma_key is None and dma_key is None and p.eng == eng:
                if eng == "tensor":
                    continue
            final.append(pid)
            p.signal = True
        o.deps = final
        for w in writes:
            lst = [(b, pid) for (b, pid) in self.wr.get(w.key, ()) if not _covers(w.box, b)]
            lst.append((w.box, o.id))
            self.wr[w.key] = lst
            rdd = self.rd.get(w.key)
            if rdd:
                for k in [k for k in rdd if _covers(w.box, k[0])]:
                    del rdd[k]
        for r in reads:
            self.rd.setdefault(r.key, {})[(r.box, eng if dma_key is None else "dma:" + dma_key)] = o.id
        self.ops.append(o)
        return o

    @staticmethod
    def _cost(eng, reads, writes, dma_key):
        def fsz(ap):
            n = 1
            for d in ap.shape[1:]:
                n *= int(d)
            return n
        if dma_key is not None:
            ap = writes[0].ap
            nbytes = fsz(ap) * int(ap.shape[0]) * (4 if ap.dtype == F32 else 2)
            return 2500.0 + nbytes / 200.0
        if eng == "tensor":
            rhs = reads[1].ap
            passes = 4 if rhs.dtype == F32 else 1
            return max(64, fsz(rhs)) * passes * 0.46 + 8.0
        n = fsz(writes[0].ap) if writes else 64
        if eng == "vector":
            return 130.0 + n * 1.15
        if eng == "scalar":
            return 240.0 + n * 0.95
        return 320.0 + n * 2.2

    def schedule(self, window=64):
        import heapq
        n = len(self.ops)
        ops = self.ops
        queues = {e: [o.id for o in ops if o.eng == e] for e in ENGINES}
        head = {e: 0 for e in ENGINES}
        placed = [False] * n
        finish = [0.0] * n
        pos = [0] * n
        for e in ENGINES:
            for k_, oid in enumerate(queues[e]):
                pos[oid] = k_
        hp = {e: [oid for oid in queues[e] if ops[oid].prio < 0] for e in ENGINES}
        hp_head = {e: 0 for e in ENGINES}
        tcur = {e: 0.0 for e in ENGINES}
        order = []
        remaining = n
        while remaining:
            best = None
            for e in ENGINES:
                q = queues[e]
                h = head[e]
                while h < len(q) and placed[q[h]]:
                    h += 1
                head[e] = h
                cnt = 0
                i = h
                hl = hp[e]
                j = hp_head[e]
                while j < len(hl) and placed[hl[j]]:
                    j += 1
                hp_head[e] = j
                got = False
                for jj in range(j, min(j + 12, len(hl))):
                    oid = hl[jj]
                    if placed[oid]:
                        continue
                    if pos[oid] - h > (900 if e == "tensor" else 250):
                        break
                    o = ops[oid]
                    ok = True
                    rdy = 0.0
                    for d in o.alld:
                        if not placed[d]:
                            ok = False
                            break
                        f = finish[d] + (60.0 if ops[d].eng == e and ops[d].dma_key is None else 350.0)
                        if f > rdy:
                            rdy = f
                    if ok and rdy <= tcur[e] + 30.0:
                        st = rdy if rdy > tcur[e] else tcur[e]
                        key = (st, -1)
                        if best is None or key < best[0]:
                            best = (key, e, oid)
                        got = True
                        break
                if got:
                    continue
                while i < len(q) and cnt < (window * 5 if e == "tensor" else window):
                    oid = q[i]
                    i += 1
                    if placed[oid]:
                        continue
                    cnt += 1
                    o = ops[oid]
                    ok = True
                    rdy = 0.0
                    for d in o.alld:
                        if not placed[d]:
                            ok = False
                            break
                        f = finish[d] + (60.0 if ops[d].eng == e and ops[d].dma_key is None else 350.0)
                        if f > rdy:
                            rdy = f
                    if not ok:
                        continue
                    st = rdy if rdy > tcur[e] else tcur[e]
                    key = (st, oid)
                    if best is None or key < best[0]:
                        best = (key, e, oid)
                    if rdy <= tcur[e]:
                        break
            assert best is not None, "scheduler stuck"
            (st, _), e, oid = best
            o = ops[oid]
            placed[oid] = True
            if o.dma_key is not None:
                tcur[e] = st + 60.0
                finish[oid] = st + o.cost
            else:
                tcur[e] = st + o.cost
                finish[oid] = tcur[e]
            order.append(oid)
            remaining -= 1
        self.sched_order = order
        self.est_ns = max(finish) if n else 0.0
        self.finish_t = finish

    def dma(self, q, out, in_, key):
        return self.op(q, lambda e, o=out.ap, i=in_.ap: e.dma_start(out=o, in_=i),
                       reads=[in_], writes=[out], dma_key=key)

    def emit(self):
        nc = self.nc
        sems = {}
        cnt = {}
        known = {e: {} for e in ENGINES}
        snaps = {}
        nsem = [0]

        def new_sem(base):
            nsem[0] += 1
            k = "%s_%d" % (base, nsem[0])
            sems[k] = nc.alloc_semaphore(k)
            return k

        seq = [self.ops[i] for i in self.sched_order] if getattr(self, "sched_order", None) else self.ops
        for o in seq:
            kn = known[o.eng]
            waits = []
            for pid in sorted(o.deps, reverse=True):
                p = self.ops[pid]
                sk, val = p.ev
                if kn.get(sk, 0) >= val:
                    continue
                waits.append((sk, val))
                kn[sk] = val
                for k2, v2 in snaps[pid].items():
                    if kn.get(k2, 0) < v2:
                        kn[k2] = v2
            o.waits = waits
            if o.signal:
                cname = ("dma:" + o.dma_key) if o.dma_key is not None else o.eng
                inc = 16 if o.dma_key is not None else 1
                sk, v = cnt.get(cname, (None, 0))
                if sk is None or v + inc > SEM_LIMIT:
                    sk, v = new_sem(cname.replace(":", "_")), 0
                v += inc
                cnt[cname] = (sk, v)
                o.ev = (sk, v)
                snaps[o.id] = dict(kn)
        self.n_sems = nsem[0]
        self.sem_final = dict(cnt)
        per_eng = {e: [o for o in seq if o.eng == e] for e in ENGINES}
        final_waits = [(sk, v) for cname, (sk, v) in cnt.items() if cname.startswith("dma:")]

        def run(e, eng):
            for o in per_eng[e]:
                for sk, val in o.waits:
                    eng.wait_ge(sems[sk], val)
                ins = o.fn(eng)
                if o.signal:
                    ins.then_inc(sems[o.ev[0]], 16 if o.dma_key is not None else 1)
            if e == "sync":
                for sk, v in final_waits:
                    eng.wait_ge(sems[sk], v)

        with nc.Block() as block:
            @block.tensor
            def _(eng):
                run("tensor", eng)

            @block.vector
            def _(eng):
                run("vector", eng)

            @block.scalar
            def _(eng):
                run("scalar", eng)

            @block.gpsimd
            def _(eng):
                run("gpsimd", eng)

            @block.sync
            def _(eng):
                run("sync", eng)


class SubT:
    def __init__(self, parent, off, shape, f32=False, p0=0):
        self.parent, self.off, self.shape = parent, off, tuple(shape)
        self.mul = 2 if f32 else 1
        self.p0 = p0
        n = 1
        for s_ in shape[1:]:
            n *= s_
        self.size = n * self.mul
        strides = []
        acc = 1
        for s_ in reversed(shape[1:]):
            strides.append(acc)
            acc *= s_
        self.strides = tuple(reversed(strides))
        ap = parent.base[p0:p0 + shape[0], off:off + self.size]
        if f32:
            ap = ap.bitcast(F32)
        if len(shape) == 3:
            ap = ap.rearrange("p (a b) -> p a b", a=shape[1])
        elif len(shape) == 4:
            ap = ap.rearrange("p (a b c) -> p a b c", a=shape[1], b=shape[2])
        self.base = ap

    def __getitem__(self, idx):
        if not isinstance(idx, tuple):
            idx = (idx,)
        idx = tuple(idx) + (slice(None),) * (len(self.shape) - len(idx))
        rng = []
        for i, n in zip(idx, self.shape):
            if isinstance(i, slice):
                lo = 0 if i.start is None else i.start
                hi = n if i.stop is None else i.stop
                assert 0 <= lo < hi <= n, (idx, self.shape)
                rng.append((lo, hi))
            else:
                assert 0 <= i < n, (idx, self.shape)
                rng.append((i, i + 1))
        flo = self.off + self.mul * sum(l * s_ for (l, _h), s_ in zip(rng[1:], self.strides))
        fhi = self.off + self.mul * (sum((h - 1) * s_ for (_l, h), s_ in zip(rng[1:], self.strides)) + 1)
        return V(self.base[idx], self.parent.name, ((self.p0 + rng[0][0], self.p0 + rng[0][1]), (flo, fhi)))


D = 1024
KC = 8
DFF = 2816
HC = 22
NMOD = 9
T = 256
A2N = 12544
EPS = 1e-6
DIN = 2336


class Cfg:
    def __init__(self, nseq=2, nlat=4096, nctx=256, depth=2, stages=None):
        self.nseq, self.nlat, self.nctx, self.depth = nseq, nlat, nctx, depth
        self.stages = stages


class Builder:
    def __init__(self, cfg):
        self.cfg = cfg
        nc = bass.Bass("TRN2", target_bir_lowering=False)
        self.nc = nc
        P = Prog(nc)
        self.P = P
        L, S = cfg.depth, cfg.nseq
        self.ncol = S + 1
        NCOL = 4
        self.xT = P.dram("xT", [S, KC, 128, cfg.nlat], F32, kind="ExternalInput")
        self.ctxT = P.dram("ctxT", [S, KC, 128, cfg.nctx], F32, kind="ExternalInput")
        self.cT = P.dram("cT", [128, KC, NCOL], F32, kind="ExternalInput")
        self.w_ada = P.dram("w_ada", [L, 36, 128, KC, 256], F32, kind="ExternalInput")
        self.b_ada = P.dram("b_ada", [128, L, 72], F32, kind="ExternalInput")
        self.norm_g = P.dram("norm_g", [128, L, 6, KC], F32, kind="ExternalInput")
        self.ffn_up = P.dram("ffn_up", [L, 2, 128, KC * 2 * DFF], F32, kind="ExternalInput")
        self.ffn_down = P.dram("ffn_down", [L, 2, 128, HC * D], F32, kind="ExternalInput")
        self.outT = P.dram("outT", [S, KC, 128, cfg.nlat], F32, kind="ExternalOutput")
        self.X = P.dram("Xs", [S, KC, 128, cfg.nlat], F32)
        self.C = P.dram("Cs", [S, KC, 128, cfg.nctx], F32)
        self.declare_mixer_io()
        self.W = P.sbuf("W", [128, KC * 2 * DFF + HC * D], BF16)
        self.w_up = SubT(self.W, 0, (128, KC, 2 * DFF))
        self.w_down = SubT(self.W, KC * 2 * DFF, (128, HC, D))
        self.xin = [P.sbuf("xin%d" % i, [128, KC, T], F32) for i in range(2)]
        self.hin = [P.sbuf("hin%d" % i, [128, KC, T], BF16) for i in range(2)]
        self.scr = [P.sbuf("scr%d" % i, [128, T], F32) for i in range(3)]
        self.rs = [P.sbuf("rs%d" % i, [128, T], F32) for i in range(2)]
        self.A2 = P.sbuf("A2", [128, A2N], BF16)
        self.s_sb = [SubT(self.A2, i * HC * T, (128, HC, T)) for i in range(2)]
        o_ = 2 * HC * T
        self.sqA = [SubT(self.A2, o_ + i * T, (128, T)) for i in range(2)]
        self.sqC = [SubT(self.A2, o_ + (2 + i) * T, (128, T)) for i in range(2)]
        self.saB = SubT(self.A2, o_ + 4 * T, (128, T))
        assert o_ + 5 * T <= A2N
        self.ffn_mode = False
        self.y_sb = P.sbuf("y_sb", [128, KC, T], F32)
        self.ones = P.sbuf("ones", [128, 128], F32)
        self.ones_bf = P.sbuf("ones_bf", [128, 128], BF16)
        self.sc = P.sbuf("sc", [128, KC, NCOL], F32)
        self.mods = P.sbuf("mods", [128, L, 72, NCOL], F32)
        self.badd = P.sbuf("badd", [128, L, 72], F32)
        self.gn = P.sbuf("gn", [128, L, 6, KC], F32)
        self.modA = P.sbuf("modA", [128, L, 3, KC, NCOL], F32)
        self.modG = P.sbuf("modG", [128, L, 3, KC, NCOL], F32)
        self.pb = [P.psum("pb%d" % i, [128, 512], F32) for i in range(8)]
        self.ps_stat = self.pb[0]
        self.ps_a = [self.pb[1], self.pb[2]]
        self.ps_b = [self.pb[3], self.pb[4]]
        self.ps_y = [self.pb[5], self.pb[6]]
        self.ps_m = self.pb[7]
        self.alloc_mixer()
        self.cnt = {}

    def scr_bf(self, kind=None):
        if self.ffn_mode and kind == "A":
            return self.sqA[self.rot("sqA", 2)][:, :]
        if self.ffn_mode and kind == "C":
            return self.sqC[self.rot("sqC", 2)][:, :]
        t = self.scr[self.rot("scr", 3)][:, :]
        return t.w(t.ap.bitcast(BF16)[:, 0:T])

    def scr_f(self, kind=None):
        if self.ffn_mode and kind == "A":
            return self.scr[self.rot("tmpA", 2)][:, :]
        if self.ffn_mode and kind == "C":
            return self.scr[2][:, :]
        return self.scr[self.rot("scr", 3)][:, :]

    def rot(self, name, n):
        v = self.cnt.get(name, 0)
        self.cnt[name] = v + 1
        return v % n

    def setup(self):
        P = self.P
        ones = self.ones[:, :]
        P.op("vector", lambda e: e.memset(ones.ap, 1.0), writes=[ones])
        onesb = self.ones_bf[:, :]
        P.op("vector", lambda e: e.memset(onesb.ap, 1.0), writes=[onesb])
        P.dma("sync", self.sc[:, :, :], self.cT[:, :, :], "small")
        P.dma("sync", self.badd[:, :, :], self.b_ada[:, :, :], "small")
        P.dma("sync", self.gn[:, :, :, :], self.norm_g[:, :, :, :], "small")
        sc = self.sc[:, :, :]
        P.op("scalar", lambda e: e.activation(out=sc.ap, in_=sc.ap, func=AF.Silu), reads=[sc], writes=[sc])

    def adaln(self):
        P = self.P
        L = self.cfg.depth
        for l in range(L):
            for u in range(36):
                slot = self.rot("xin", 2)
                st = self.xin[slot]
                P.dma("sync", st[:, :, :], self.w_ada[l, u, :, :, :], "xin%d" % slot)
                for j in range(2):
                    col = (u * 2 + j) * 4
                    out = self.ps_m[:, col:col + 4]
                    for kc in range(KC):
                        lhs = st[:, kc, j * 128:(j + 1) * 128]
                        rhs = self.sc[:, kc, :]
                        P.op("tensor",
                             lambda e, o=out.ap, a=lhs.ap, b=rhs.ap, k=kc: e.matmul(
                                 o, lhsT=a, rhs=b, start=(k == 0), stop=(k == KC - 1)),
                             reads=[lhs, rhs], writes=[out])
            psv = self.ps_m[:, 0:288]
            mo = self.mods[:, l, :, :]
            bb = self.badd[:, l, :]
            P.op("vector",
                 lambda e, o=mo.ap, p=psv.ap, b=bb.ap: e.tensor_tensor(
                     out=o, in0=p.rearrange("p (a b) -> p a b", b=4),
                     in1=b.unsqueeze(2).broadcast_to([128, 72, 4]), op=ALU.add),
                 reads=[psv, bb], writes=[mo])
            for i in range(3):
                scale = self.mods[:, l, (3 * i + 1) * KC:(3 * i + 2) * KC, :]
                gate = self.mods[:, l, (3 * i + 2) * KC:(3 * i + 3) * KC, :]
                g_in = self.gn[:, l, 2 * i, :]
                g_out = self.gn[:, l, 2 * i + 1, :]
                A = self.modA[:, l, i, :, :]
                G = self.modG[:, l, i, :, :]
                wgt = 1.0 if i == 1 else 0.5
                P.op("vector",
                     lambda e, o=A.ap, s=scale.ap, g=g_in.ap: e.scalar_tensor_tensor(
                         out=o, in0=s, scalar=1.0, in1=g.unsqueeze(2).broadcast_to([128, KC, 4]),
                         op0=ALU.add, op1=ALU.mult),
                     reads=[scale, g_in], writes=[A])
                P.op("vector",
                     lambda e, o=G.ap, s=gate.ap, g=g_out.ap, w=wgt: e.scalar_tensor_tensor(
                         out=o, in0=s, scalar=w, in1=g.unsqueeze(2).broadcast_to([128, KC, 4]),
                         op0=ALU.mult, op1=ALU.mult),
                     reads=[gate, g_out], writes=[G])

    def shift(self, l, i, kc, col):
        return self.mods[:, l, 3 * i * KC + kc, col:col + 1]

    def load_ffn_weights(self, l, which):
        P = self.P
        CH = 2048
        n_up = KC * 2 * DFF
        for c0 in range(0, n_up, CH):
            c1 = min(n_up, c0 + CH)
            dst = V(self.W.base[:, c0:c1], "W", ((0, 128), (c0, c1)))
            P.dma("gpsimd", dst, self.ffn_up[l, which, :, c0:c1], "wload%d" % self.rot("wl", 4))
        n_dn = HC * D
        for c0 in range(0, n_dn, CH):
            c1 = min(n_dn, c0 + CH)
            dst = V(self.W.base[:, n_up + c0:n_up + c1], "W", ((0, 128), (n_up + c0, n_up + c1)))
            P.dma("gpsimd", dst, self.ffn_down[l, which, :, c0:c1], "wload%d" % self.rot("wl", 4))

    def rstd_from_stat(self, ps, rs):
        P = self.P
        P.op("scalar", lambda e, o=rs.ap, i=ps.ap: e.activation(out=o, in_=i, func=AF.Ln, bias=EPS, scale=1.0 / D),
             reads=[ps], writes=[rs])
        P.op("scalar", lambda e, o=rs.ap: e.activation(out=o, in_=o, func=AF.Exp, scale=-0.5),
             reads=[rs], writes=[rs])

    def norm_in(self, xin, l, i, col, hin):
        P = self.P
        st = self.ps_stat[:, 0:T]
        for kc in range(KC):
            sq = self.scr_bf("A")
            xk = xin[:, kc, :]
            P.op("scalar", lambda e, o=sq.ap, i_=xk.ap: e.activation(out=o, in_=i_, func=AF.Square),
                 reads=[xk], writes=[sq])
            P.op("tensor", lambda e, o=st.ap, a=self.ones_bf[:, :].ap, b=sq.ap, k=kc: e.matmul(
                o, lhsT=a, rhs=b, start=(k == 0), stop=(k == KC - 1)),
                reads=[self.ones_bf[:, :], sq], writes=[st])
        rs = self.rs[self.rot("rs", 2)][:, :]
        self.rstd_from_stat(st, rs)
        for kc in range(KC):
            tmp = self.scr_f("A")
            xk = xin[:, kc, :]
            P.op("vector", lambda e, o=tmp.ap, a=xk.ap, b=rs.ap: e.tensor_tensor(out=o, in0=a, in1=b, op=ALU.mult),
                 reads=[xk, rs], writes=[tmp])
            A = self.modA[:, l, i, kc, col:col + 1]
            sh = self.shift(l, i, kc, col)
            hk = hin[:, kc, :]
            if kc % 2 == 0:
                P.op("gpsimd", lambda e, o=hk.ap, t=tmp.ap, a=A.ap, s=sh.ap: e.tensor_scalar(
                    out=o, in0=t, scalar1=a, scalar2=s, op0=ALU.mult, op1=ALU.add),
                    reads=[tmp, A, sh], writes=[hk])
            else:
                P.op("scalar", lambda e, o=hk.ap, t=tmp.ap, a=A.ap, s=sh.ap: e.activation(
                    out=o, in_=t, func=AF.Identity, bias=s, scale=a),
                    reads=[tmp, A, sh], writes=[hk])

    def norm_out(self, y_sb, xin, l, i, col, st=None):
        P = self.P
        st = self.ps_stat[:, 0:T] if st is None else st
        rs = self.rs[self.rot("rs", 2)][:, :]
        self.rstd_from_stat(st, rs)
        for kc in range(KC):
            tmp = self.scr_f("C")
            yk = y_sb[:, kc, :]
            xk = xin[:, kc, :]
            P.op("vector", lambda e, o=tmp.ap, a=yk.ap, b=rs.ap: e.tensor_tensor(out=o, in0=a, in1=b, op=ALU.mult),
                 reads=[yk, rs], writes=[tmp])
            G = self.modG[:, l, i, kc, col:col + 1]
            P.op("vector", lambda e, o=xk.ap, t=tmp.ap, g=G.ap: e.scalar_tensor_tensor(
                out=o, in0=t, scalar=g, in1=o, op0=ALU.mult, op1=ALU.add),
                reads=[tmp, G, xk], writes=[xk])

    def evac_with_stats(self, ps, dst, k, n, st=None):
        P = self.P
        st = self.ps_stat[:, 0:T] if st is None else st
        P.op("vector", lambda e, o=dst.ap, i_=ps.ap: e.tensor_copy(out=o, in_=i_), reads=[ps], writes=[dst])
        sq = self.scr_bf("C")
        P.op("scalar", lambda e, o=sq.ap, i_=dst.ap: e.activation(out=o, in_=i_, func=AF.Square),
             reads=[dst], writes=[sq])
        P.op("tensor", lambda e, o=st.ap, a=self.ones_bf[:, :].ap, b=sq.ap: e.matmul(
            o, lhsT=a, rhs=b, start=(k == 0), stop=(k == n - 1)),
            reads=[self.ones_bf[:, :], sq], writes=[st])

    def ffn_block(self, l, i, col, src, dst):
        P = self.P
        slot = self.rot("xin", 2)
        xin = self.xin[slot]
        bid = self.rot("blk", 1 << 30)
        P.cur_tag = "b%d:A" % bid
        P.cur_prio = -1
        P.dma("sync", xin[:, :, :], src.w(src.ap.rearrange("k p t -> p k t")), "xin%d" % slot)
        hin = self.hin[self.rot("hin", 2)]
        self.norm_in(xin, l, i, col, hin)
        P.cur_prio = 0
        P.cur_tag = "b%d:B" % bid
        s_sb = self.s_sb[self.rot("s", 2)]
        dbg = getattr(self, "dbg", "all")
        for hc in range(HC if dbg != "up1" else 1):
            pab = self.pb[1 + self.rot("ps_ab", 4)]
            pa = pab[:, 0:T]
            pb = pab[:, T:2 * T]
            for (ps, off) in ((pa, 0), (pb, DFF)):
                for kc in range(KC):
                    lhs = self.w_up[:, kc, off + hc * 128: off + (hc + 1) * 128]
                    rhs = hin[:, kc, :]
                    P.op("tensor", lambda e, o=ps.ap, a=lhs.ap, b=rhs.ap, k=kc: e.matmul(
                        o, lhsT=a, rhs=b, start=(k == 0), stop=(k == KC - 1)),
                        reads=[lhs, rhs], writes=[ps])
            sa = self.saB[:, :]
            P.op("scalar", lambda e, o=sa.ap, i_=pa.ap: e.activation(out=o, in_=i_, func=AF.Silu),
                 reads=[pa], writes=[sa])
            sk = s_sb[:, hc, :]
            P.op("vector", lambda e, o=sk.ap, a=sa.ap, b=pb.ap: e.tensor_tensor(out=o, in0=a, in1=b, op=ALU.mult),
                 reads=[sa, pb], writes=[sk])
        P.cur_tag = "b%d:C" % bid
        for dc in range(KC if dbg in ("all", "down") else 0):
            py = self.pb[5 + self.rot("ps_y", 2)][:, 0:T]
            for hc in range(HC):
                lhs = self.w_down[:, hc, dc * 128:(dc + 1) * 128]
                rhs = s_sb[:, hc, :]
                P.op("tensor", lambda e, o=py.ap, a=lhs.ap, b=rhs.ap, k=hc: e.matmul(
                    o, lhsT=a, rhs=b, start=(k == 0), stop=(k == HC - 1)),
                    reads=[lhs, rhs], writes=[py])
            self.evac_with_stats(py, self.y_sb[:, dc, :], dc, KC, st=self.pb[7][:, 0:T])
        if dbg == "all":
            self.norm_out(self.y_sb, xin, l, i, col, st=self.pb[7][:, 0:T])
        P.dma("sync", dst.w(dst.ap.rearrange("k p t -> p k t")), xin[:, :, :], "xout%d" % slot)

    def ffn_sublayer(self, l, i, which, lat_src, lat_dst, do_ctx=True, ctx_src=None):
        cfg = self.cfg
        self.load_ffn_weights(l, which)
        self.ffn_mode = True
        for s in range(cfg.nseq):
            if do_ctx:
                cs = (ctx_src if ctx_src is not None else self.C)
                self.ffn_block(l, i, cfg.nseq, cs[s, :, :, 0:T], self.C[s, :, :, 0:T])
            for t0 in range(0, cfg.nlat, T):
                self.ffn_block(l, i, s, lat_src[s, :, :, t0:t0 + T], lat_dst[s, :, :, t0:t0 + T])
        self.ffn_mode = False


def _fm(a):
    B, N, Dm = a.shape
    return np.ascontiguousarray(a.transpose(0, 2, 1).reshape(B, Dm // 128, 128, N))


def shared_weight_maps(inp, L):
    m = {}
    wa = inp["w_ada"][:L]
    m["w_ada"] = np.ascontiguousarray(
        wa.reshape(L, KC, 128, 36, 256).transpose(0, 3, 2, 1, 4))
    m["b_ada"] = np.ascontiguousarray(inp["b_ada"][:L].reshape(L, 72, 128).transpose(2, 0, 1))
    m["norm_g"] = np.ascontiguousarray(inp["norm_g"][:L].reshape(L, 6, KC, 128).transpose(3, 0, 1, 2))
    ups, dns = [], []
    for l in range(L):
        u, d = [], []
        for nm_u, nm_d in (("ffn1_up", "ffn1_down"), ("ffn2_up", "ffn2_down")):
            wu = inp[nm_u][l]
            u.append(wu.reshape(KC, 128, 2 * DFF).transpose(1, 0, 2).reshape(128, KC * 2 * DFF))
            wd = inp[nm_d][l]
            d.append(wd.reshape(HC, 128, D).transpose(1, 0, 2).reshape(128, HC * D))
        ups.append(np.stack(u))
        dns.append(np.stack(d))
    m["ffn_up"] = np.ascontiguousarray(np.stack(ups))
    m["ffn_down"] = np.ascontiguousarray(np.stack(dns))
    return m


def core_maps(inp, cfg, core, shared):
    S = cfg.nseq
    b0 = core * S
    m = dict(shared)
    m["xT"] = _fm(inp["x"][b0:b0 + S])
    m["ctxT"] = _fm(inp["ctx"][b0:b0 + S])
    cols = [inp["c"][b0 + s] for s in range(S)] + [inp["c_ctx"]]
    while len(cols) < 4:
        cols.append(np.zeros_like(inp["c_ctx"]))
    cT = np.stack(cols, axis=1)
    m["cT"] = np.ascontiguousarray(cT.reshape(KC, 128, 4).transpose(1, 0, 2))
    return m


NV = 80
SCHED = True
POOLW = (2, 4, 8, 16)
LN8 = float(np.log(0.125))


def pool_tables():
    mats, idx = [], {}
    ti = np.arange(128)
    for gi, w in enumerate(POOLW):
        lo, hi = w // 2, w - 1 - w // 2
        for dl in range(-5, 6):
            ri, ci = ti // 64, ti % 64
            dr = 2 * dl + ri[:, None] - ri[None, :]
            dc = ci[:, None] - ci[None, :]
            m = ((dr >= -lo) & (dr <= hi) & (dc >= -lo) & (dc <= hi)).astype(np.float32)
            if m.any():
                idx[("lat", gi, dl)] = len(mats)
                mats.append(m)
        for dl in (-1, 0, 1):
            dt_ = 128 * dl + ti[:, None] - ti[None, :]
            m = ((dt_ >= -lo) & (dt_ <= hi)).astype(np.float32)
            if m.any():
                idx[("ctx", gi, dl)] = len(mats)
                mats.append(m)
    return np.stack(mats), idx


POOLM, POOLIDX = pool_tables()
NPM = POOLM.shape[0]


def _bm(self):
    pass


def declare_mixer_io(self):
    P, cfg = self.P, self.cfg
    L, S = cfg.depth, cfg.nseq
    self.w_in_d = P.dram("w_in", [L, 128, KC * DIN], F32, kind="ExternalInput")
    self.w_out_d = P.dram("w_out", [L, 128, KC * D], F32, kind="ExternalInput")
    self.pw_d = P.dram("conv_pw", [L, 128, 512], F32, kind="ExternalInput")
    self.poolw_d = P.dram("pool_w", [L, 64, 512], F32, kind="ExternalInput")
    self.poolM_d = P.dram("poolM", [128, NPM * 128], F32, kind="ExternalInput")
    self.masks_d = P.dram("masks", [128, 256], F32, kind="ExternalInput")
    self.cumM_d = P.dram("cumM", [128, 2, 128], F32, kind="ExternalInput")
    self.wgk_d = P.dram("wgk", [L, 32, 2, 256], F32, kind="ExternalInput")
    self.vecs_d = P.dram("vecs", [128, L, NV], F32, kind="ExternalInput")
    self.invl_d = P.dram("invc_lat", [64, 4, cfg.nlat], F32, kind="ExternalInput")
    self.invc_d = P.dram("invc_ctx", [64, 4, cfg.nctx], F32, kind="ExternalInput")
    self.OBl = P.dram("OBl", [S, 128, 4, cfg.nlat], F32)
    self.OBc = P.dram("OBc", [S, 128, 4, cfg.nctx], F32)


def alloc_mixer(self):
    P, cfg = self.P, self.cfg
    L = cfg.depth
    W = self.W
    o = [0]

    def carve(shape, f32=False, arena=None, p0=0):
        ar = arena or W
        key = id(ar)
        off = self._off.setdefault(key, 0)
        t = SubT(ar, off, shape, f32=f32, p0=p0)
        self._off[key] = off + t.size
        return t
    self._off = {}
    self.m_win = carve((128, KC, DIN))
    self.m_wout = carve((128, KC, D))
    self.m_pw = carve((128, 2, 256))
    self.m_poolw = carve((64, 4, 128))
    self.m_poolM = carve((128, NPM, 128))
    self.m_mask = carve((128, 2, 128))
    nsub = cfg.nlat // 128
    self.UT = carve((128, nsub, 256))
    self.HB = carve((128, 2, cfg.nlat + 32), f32=True)
    self.qT = carve((64, 4, T), f32=True)
    self.kT = carve((64, 4, T), f32=True)
    self.gs = carve((128, 4, T), f32=True)
    self.cat = carve((128, 8, T))
    self.uT = carve((64, 4, T), f32=True)
    assert self._off[id(W)] <= KC * 2 * DFF + HC * D, self._off[id(W)]
    A2 = self.A2
    self.ktok = carve((128, 2, 256), f32=True, arena=A2)
    self.vtok = carve((128, 2, 512), arena=A2)
    self.lraug = carve((32, T), arena=A2)
    self.sp = carve((128, 256), arena=A2)
    self.EqT = carve((64, 4, 128), f32=True, arena=A2)
    self.EkT = carve((64, 4, 128), f32=True, arena=A2)
    self.Ektok = carve((128, 256), f32=True, arena=A2)
    self.qt = carve((64, 4, 128), arena=A2)
    self.kt = carve((64, 4, 128), arena=A2)
    self.kttok = carve((128, 256), arena=A2)
    self.attm = carve((128, 4, 128), arena=A2)
    self.hs = SubT(A2, self.attm.off, (128, 2, T))
    self.Tst = carve((64, 4, 128), f32=True, arena=A2)
    self.S32 = carve((64, 4, 128), f32=True, arena=A2)
    self.Sbf = carve((64, 4, 128), arena=A2)
    self.o_sb = carve((128, 4, 128), f32=True, arena=A2)
    self.ob_sb = carve((128, 4, 128), f32=True, arena=A2)
    self.invt = SubT(A2, self.ob_sb.off, (64, 4, 128), f32=True)
    self.dpl = carve((64, 4, 128), arena=A2)
    assert self._off[id(A2)] <= A2N, self._off[id(A2)]
    self.Dbuf = P.sbuf("Dbuf", [64, 4, 4, 2], F32)
    self.Dsave = P.sbuf("Dsave", [64, 2, 4, 1], F32)
    self.Tdir = {0: self.Tst, 1: P.sbuf("Tst_b", [64, 4, 128], F32)}
    self.Dprev = {}
    self.onesc = P.sbuf("onesc", [64, 4, 1], F32)
    self.cumM = P.sbuf("cumM_sb", [128, 2, 128], BF16)
    self.wgk = P.sbuf("wgk_sb", [32, L, 2, 256], BF16)
    self.vecs = P.sbuf("vecs_sb", [128, L, NV], F32)


Builder.declare_mixer_io = declare_mixer_io
Builder.alloc_mixer = alloc_mixer


class MixerMixin:
    def mm(self, out, lhsT, rhs, start=True, stop=True, skip=False):
        self.P.op("tensor", lambda e, o=out.ap, a=lhsT.ap, b=rhs.ap: e.matmul(
            o, lhsT=a, rhs=b, start=start, stop=stop, skip_group_check=skip),
            reads=[lhsT, rhs], writes=[out])

    def act(self, out, in_, func, bias=None, scale=None, extra_reads=()):
        kw = {}
        if bias is not None:
            kw["bias"] = bias.ap if isinstance(bias, V) else bias
        if scale is not None:
            kw["scale"] = scale.ap if isinstance(scale, V) else scale
        rd = [in_] + [x for x in (bias, scale) if isinstance(x, V)] + list(extra_reads)
        self.P.op("scalar", lambda e, o=out.ap, i=in_.ap: e.activation(out=o, in_=i, func=func, **kw),
                  reads=rd, writes=[out])

    def tt(self, out, a, b, op, eng="vector", bap=None):
        b_ap = b.ap if bap is None else bap
        self.P.op(eng, lambda e, o=out.ap, x=a.ap, y=b_ap: e.tensor_tensor(out=o, in0=x, in1=y, op=op),
                  reads=[a, b], writes=[out])

    def ts(self, out, a, s1, s2, op0, op1, eng="vector"):
        rd = [a] + [x for x in (s1, s2) if isinstance(x, V)]
        g = lambda x: x.ap if isinstance(x, V) else x
        self.P.op(eng, lambda e, o=out.ap, x=a.ap: e.tensor_scalar(
            out=o, in0=x, scalar1=g(s1), scalar2=g(s2), op0=op0, op1=op1), reads=rd, writes=[out])

    def stt(self, out, a, sc, b, op0, op1, bap=None):
        rd = [a, b] + ([sc] if isinstance(sc, V) else [])
        g = lambda x: x.ap if isinstance(x, V) else x
        b_ap = b.ap if bap is None else bap
        self.P.op("vector", lambda e, o=out.ap, x=a.ap, y=b_ap: e.scalar_tensor_tensor(
            out=o, in0=x, scalar=g(sc), in1=y, op0=op0, op1=op1), reads=rd, writes=[out])

    def cp(self, out, in_, eng=None):
        if eng is None:
            eng = ("scalar", "vector", "scalar")[self.rot("cp", 3)]
        if eng == "scalar":
            self.P.op("scalar", lambda e, o=out.ap, i=in_.ap: e.activation(out=o, in_=i, func=AF.Copy),
                      reads=[in_], writes=[out])
        else:
            self.P.op(eng, lambda e, o=out.ap, i=in_.ap: e.tensor_copy(out=o, in_=i), reads=[in_], writes=[out])

    def cacc(self, cc):
        return self.y_sb[:, 2 + cc, :]

    def pp(self):
        return self.pb[1 + self.rot("pp", 2)]

    def mixer_consts(self):
        P = self.P
        P.dma("gpsimd", self.cumM[:, :, :], self.cumM_d[:, :, :], "wload%d" % self.rot("wl", 4))
        P.dma("gpsimd", self.wgk[:, :, :, :], self.wgk_d[:, :, :, :].w(self.wgk_d.base.rearrange("l p d c -> p l d c")), "wload%d" % self.rot("wl", 4))
        P.dma("sync", self.vecs[:, :, :], self.vecs_d[:, :, :], "small")
        oc = self.onesc[:, :, :]
        P.op("vector", lambda e: e.memset(oc.ap, 1.0), writes=[oc])

    def load_mixer_weights(self, l):
        P = self.P
        CH = 2048

        def ld(dst_t, src_t, src_idx, n, p=128):
            for c0 in range(0, n, CH):
                c1 = min(n, c0 + CH)
                dst = V(self.W.base[0:p, dst_t.off + c0:dst_t.off + c1], "W", ((0, p), (dst_t.off + c0, dst_t.off + c1)))
                P.dma("gpsimd", dst, src_t[src_idx + (slice(c0, c1),)], "wload%d" % self.rot("wl", 4))
        ld(self.m_win, self.w_in_d, (l, slice(None)), KC * DIN)
        ld(self.m_wout, self.w_out_d, (l, slice(None)), KC * D)
        ld(self.m_pw, self.pw_d, (l, slice(None)), 512)
        ld(self.m_poolw, self.poolw_d, (l, slice(None)), 512, p=64)
        ld(self.m_poolM, self.poolM_d, (slice(None),), NPM * 128)
        ld(self.m_mask, self.masks_d, (slice(None),), 256)

    def proj_fm(self, hin, col0, M, out):
        for kc in range(KC):
            self.mm(out, self.m_win[:, kc, col0:col0 + M], hin[:, kc, :], start=(kc == 0), stop=(kc == KC - 1))

    def proj_tok(self, hin, sub, col0, n, out):
        for kc in range(KC):
            self.mm(out, hin[:, kc, sub * 128:(sub + 1) * 128], self.m_win[:, kc, col0:col0 + n],
                    start=(kc == 0), stop=(kc == KC - 1))

    def gla_proj(self, hin, d):
        for h in range(4):
            ps = self.pp()[0:64, 0:T]
            self.proj_fm(hin, h * 64, 64, ps)
            self.cp(self.qT[:, h, :], ps)
            ps = self.pp()[0:64, 0:T]
            self.proj_fm(hin, 256 + h * 64, 64, ps)
            self.cp(self.kT[:, h, :], ps)
        ps = self.pp()[0:16, 0:T]
        self.proj_fm(hin, 1536 + 16 * d, 16, ps)
        self.cp(self.lraug[0:16, :], ps)
        for sub in range(2):
            ps = self.pp()[:, 0:256]
            self.proj_tok(hin, sub, 256, 256, ps)
            self.cp(self.ktok[:, sub, :], ps)
            ps = self.pp()[:, 0:512]
            self.proj_tok(hin, sub, 512, 512, ps)
            self.cp(self.vtok[:, sub, :], ps)

    def gla_sub(self, l, d, sub, state_only):
        P = self.P
        P.cur_prio = -1
        try:
            self._gla_sub(l, d, sub, state_only)
        finally:
            P.cur_prio = 0

    def _gla_sub(self, l, d, sub, state_only):
        P = self.P
        tok = slice(sub * 128, (sub + 1) * 128)
        pb3, pb4, pb5, pb6, pb7 = self.pb[3], self.pb[4], self.pb[5], self.pb[6], self.pb[7]
        self.mm(pb3[:, 0:256], self.lraug[0:17, tok], self.wgk[0:17, l, d, :])
        self.act(self.Ektok[:, :], pb3[:, 0:256], AF.Exp, scale=-1.0)
        self.act(self.sp[:, :], self.Ektok[:, :], AF.Ln, bias=1.0)
        self.mm(pb3[:, 256:512], self.cumM[:, d, :], self.sp[:, :])
        b4 = pb4[0:64, :]
        b4v = b4.w(b4.ap.rearrange("p (h t) -> p h t", h=4))
        for h in range(4):
            self.mm(pb4[0:64, h * 128:(h + 1) * 128], self.sp[:, h * 64:(h + 1) * 64], self.cumM[:, d, :])
        self.act(self.Ektok[:, :], pb3[:, 256:512], AF.Exp, scale=-1.0)
        self.act(self.EqT[:, :, :], b4v, AF.Exp, bias=LN8)
        self.act(self.EkT[:, :, :], b4v, AF.Exp, scale=-1.0)
        slot = self.rot("Dslot", 4)
        for c in range(2):
            col = c * 64 + (63 if d == 0 else 0)
            src = b4.w(b4v.ap[:, :, col:col + 1])
            self.act(self.Dbuf[:, slot, :, c:c + 1], src, AF.Exp)
        self.tt(self.qt[:, :, :], self.qT[:, :, tok], self.EqT[:, :, :], ALU.mult)
        self.tt(self.kt[:, :, :], self.kT[:, :, tok], self.EkT[:, :, :], ALU.mult)
        self.tt(self.kttok[:, :], self.ktok[:, sub, :], self.Ektok[:, :], ALU.mult)
        if not state_only:
            p5 = pb5[:, :]
            for h in range(4):
                self.mm(pb5[:, h * 128:(h + 1) * 128], self.kt[:, h, :], self.qt[:, h, :])
            mk = self.m_mask[:, d, :]
            self.tt(self.attm[:, :, :], p5.w(p5.ap.rearrange("p (h t) -> p h t", h=4)), mk, ALU.mult,
                    bap=mk.ap.unsqueeze(1).broadcast_to([128, 4, 128]))
        first = True
        for c in ((0, 1) if d == 0 else (1, 0)):
            dprev = self.Dprev[d]
            self.tt(self.S32[:, :, :], self.Tdir[d][:, :, :], dprev, ALU.mult,
                    bap=dprev.ap.broadcast_to([64, 4, 128]))
            if not state_only:
                self.cp(self.Sbf[:, :, :], self.S32[:, :, :], eng="scalar")
                for h in range(4):
                    self.mm(pb6[:, h * 128 + c * 64:h * 128 + (c + 1) * 64], self.Sbf[:, h, :],
                            self.qt[:, h, c * 64:(c + 1) * 64], start=first, stop=False, skip=True)
                    first = False
            for h in range(4):
                self.mm(pb7[0:64, h * 128:(h + 1) * 128], self.kttok[c * 64:(c + 1) * 64, h * 64:(h + 1) * 64],
                        self.vtok[c * 64:(c + 1) * 64, sub, h * 128:(h + 1) * 128], skip=True)
            p7 = pb7[0:64, :]
            self.tt(self.Tdir[d][:, :, :], self.S32[:, :, :], p7.w(p7.ap.rearrange("p (h t) -> p h t", h=4)), ALU.add)
            self.Dprev[d] = self.Dbuf[:, slot, :, c:c + 1]
        if not state_only:
            for h in range(4):
                self.mm(pb6[:, h * 128:(h + 1) * 128], self.vtok[:, sub, h * 128:(h + 1) * 128], self.attm[:, h, :],
                        start=False, stop=True, skip=True)

    def gla_reset(self, d):
        t = self.Tdir[d][:, :, :]
        self.P.op("vector", lambda e: e.memset(t.ap, 0.0), writes=[t])
        self.Dprev[d] = self.onesc[:, :, :]

    def keep_D(self, d):
        self.cp(self.Dsave[:, d, :, :], self.Dprev[d], eng="vector")
        self.Dprev[d] = self.Dsave[:, d, :, :]

    def mixer_block_B(self, l, s, kind, t0, col, need_out):
        P, cfg = self.P, self.cfg
        src_t = self.C if kind == "ctx" else self.X
        src = src_t[s, :, :, t0:t0 + T]
        slot = self.rot("xin", 2)
        xin = self.xin[slot]
        P.cur_prio = -1
        P.dma("sync", xin[:, :, :], src.w(src.ap.rearrange("k p t -> p k t")), "xin%d" % slot)
        hin = self.hin[self.rot("hin", 2)]
        self.norm_in(xin, l, 1, col, hin)
        P.cur_prio = 0
        self.gla_proj(hin, 1)
        if need_out:
            for sub in range(2):
                ps = self.pp()[:, 0:256]
                self.proj_tok(hin, sub, 1568, 256, ps)
                self.cp(self.UT[:, t0 // 128 + sub, :], ps)
            for cc in range(2):
                pa = self.pp()[:, 0:T]
                self.proj_fm(hin, 1824 + cc * 128, 128, pa)
                pg = self.pp()[:, 0:T]
                self.proj_fm(hin, 2080 + cc * 128, 128, pg)
                sg = self.scr[self.rot("scr", 3)][:, :]
                self.act(sg, pg, AF.Sigmoid)
                self.tt(self.HB[:, cc, 16 + t0:16 + t0 + T], pa, sg, ALU.mult)
        OB = self.OBc if kind == "ctx" else self.OBl
        for sub in (1, 0):
            self.gla_sub(l, 1, sub, state_only=not need_out)
            if need_out:
                p6 = self.pb[6][:, :]
                self.cp(self.o_sb[:, :, :], p6.w(p6.ap.rearrange("p (h t) -> p h t", h=4)), eng="scalar")
                g0 = t0 + sub * 128
                P.dma("sync", OB[s, :, :, g0:g0 + 128], self.o_sb[:, :, :], "obst")

    def mixer_block_F(self, l, s, kind, t0, col, need_out, dst_t):
        P, cfg = self.P, self.cfg
        src_t = self.C if kind == "ctx" else self.X
        n_tok = cfg.nctx if kind == "ctx" else cfg.nlat
        src = src_t[s, :, :, t0:t0 + T]
        slot = self.rot("xin", 2)
        xin = self.xin[slot]
        P.cur_prio = -1
        P.dma("sync", xin[:, :, :], src.w(src.ap.rearrange("k p t -> p k t")), "xin%d" % slot)
        hin = self.hin[self.rot("hin", 2)]
        self.norm_in(xin, l, 1, col, hin)
        P.cur_prio = 0
        self.gla_proj(hin, 0)
        if not need_out:
            for sub in (0, 1):
                self.gla_sub(l, 0, sub, state_only=True)
            return
        vec = lambda j: self.vecs[:, l, j:j + 1]
        for h in range(4):
            ps = self.pp()[:, 0:T]
            self.proj_fm(hin, 1024 + h * 128, 128, ps)
            self.act(self.gs[:, h, :], ps, AF.Silu)
        for gi in range(4):
            ps = self.pp()[0:64, 0:T]
            self.proj_fm(hin, 1568 + gi * 64, 64, ps)
            self.cp(self.uT[:, gi, :], ps)
        OB = self.OBc if kind == "ctx" else self.OBl
        invd = self.invc_d if kind == "ctx" else self.invl_d
        nsub_tot = n_tok // 128
        for sub in (0, 1):
            tok = slice(sub * 128, (sub + 1) * 128)
            g0 = t0 + sub * 128
            P.dma("sync", self.ob_sb[:, :, :], OB[s, :, :, g0:g0 + 128], "obld")
            self.gla_sub(l, 0, sub, state_only=False)
            p6 = self.pb[6][:, :]
            self.tt(self.o_sb[:, :, :], p6.w(p6.ap.rearrange("p (h t) -> p h t", h=4)), self.ob_sb[:, :, :], ALU.add)
            self.act(self.attm[:, :, :], self.o_sb[:, :, :], AF.Square)
            for h in range(4):
                self.mm(self.pb[0][:, h * 128:(h + 1) * 128], self.ones_bf[:, :], self.attm[:, h, :])
            r4 = self.y_sb[:, 0:2, :]
            r4f = r4.w(r4.ap.rearrange("p a b -> p (a b)"))
            self.act(r4f, self.pb[0][:, :], AF.Ln, bias=EPS, scale=1.0 / 128)
            self.act(r4f, r4f, AF.Exp, scale=-0.5)
            self.tt(self.o_sb[:, :, :], self.o_sb[:, :, :], r4.w(r4.ap.rearrange("p a (b t) -> p (a b) t", b=2)), ALU.mult)
            self.stt(self.cat[:, 0:4, tok], self.o_sb[:, :, :], vec(0), self.gs[:, :, tok], ALU.mult, ALU.mult)
            j = g0 // 128
            P.dma("sync", self.invt[:, :, :], invd[:, :, g0:g0 + 128], "invt")
            for gi in range(4):
                dls = [dl for dl in range(-5, 6) if (kind, gi, dl) in POOLIDX and 0 <= j + dl < nsub_tot]
                for n_, dl in enumerate(dls):
                    self.mm(self.pb[7][0:64, gi * 128:(gi + 1) * 128], self.UT[:, j + dl, gi * 64:(gi + 1) * 64],
                            self.m_poolM[:, POOLIDX[(kind, gi, dl)], :], start=(n_ == 0), stop=(n_ == len(dls) - 1))
            p7 = self.pb[7][0:64, :]
            self.tt(self.invt[:, :, :], p7.w(p7.ap.rearrange("p (h t) -> p h t", h=4)), self.invt[:, :, :], ALU.mult)
            self.tt(self.dpl[:, :, :], self.invt[:, :, :], self.uT[:, :, tok], ALU.subtract)
            for cc in range(2):
                ps = self.pp()[:, 0:128]
                self.mm(ps, self.m_poolw[:, 2 * cc, :], self.dpl[:, 2 * cc, :], start=True, stop=False)
                self.mm(ps, self.m_poolw[:, 2 * cc + 1, :], self.dpl[:, 2 * cc + 1, :], start=False, stop=True)
                self.ts(self.cat[:, 4 + cc, tok], ps, vec(1 + cc), None, ALU.mult, ALU.bypass)
        NDVE = 7
        for cc in range(2):
            acc = self.cacc(cc)
            acc2 = self.y_sb[:, 4, :]
            tmpc = self.y_sb[:, 7, :]
            for jt in range(31):
                hsl = self.HB[:, cc, t0 + jt + 1:t0 + jt + 1 + T]
                wj = vec(3 + cc * 31 + jt)
                if jt == 0:
                    self.ts(acc, hsl, wj, vec(65 + cc), ALU.mult, ALU.add)
                elif cc == 0 or jt < NDVE:
                    self.stt(acc, hsl, wj, acc, ALU.mult, ALU.add)
                elif jt == NDVE:
                    self.ts(acc2, hsl, wj, 0.0, ALU.mult, ALU.add, eng="gpsimd")
                else:
                    self.ts(tmpc, hsl, wj, 0.0, ALU.mult, ALU.add, eng="gpsimd")
                    self.tt(acc2, acc2, tmpc, ALU.add, eng="gpsimd")
            if cc == 1:
                self.tt(acc, acc, acc2, ALU.add)
        st = self.pb[0]
        for cc in range(2):
            self.mm(st[:, 0:T], self.ones[:, :], self.cacc(cc), start=(cc == 0), stop=(cc == 1))
        sqs = []
        for cc in range(2):
            sq = self.scr_bf()
            self.act(sq, self.cacc(cc), AF.Square)
            sqs.append(sq)
        for cc in range(2):
            self.mm(st[:, T:2 * T], self.ones_bf[:, :], sqs[cc], start=(cc == 0), stop=(cc == 1))
        mean = self.scr[self.rot("scr", 3)][:, :]
        msq = self.scr[self.rot("scr", 3)][:, :]
        self.act(mean, st[:, 0:T], AF.Copy, scale=1.0 / 256)
        self.act(msq, st[:, 0:T], AF.Square, scale=1.0 / 256)
        rs = self.rs[self.rot("rs", 2)][:, :]
        self.stt(rs, st[:, T:2 * T], 1.0 / 256, msq, ALU.mult, ALU.subtract)
        self.act(rs, rs, AF.Ln, bias=EPS)
        self.act(rs, rs, AF.Exp, scale=-0.5)
        for cc in range(2):
            acc = self.cacc(cc)
            self.tt(acc, acc, mean, ALU.subtract)
            self.tt(acc, acc, rs, ALU.mult)
            self.ts(acc, acc, vec(67 + cc), vec(69 + cc), ALU.mult, ALU.add, eng="gpsimd")
            self.act(self.hs[:, cc, :], acc, AF.Silu)
        for co in range(2):
            ps = self.pp()[:, 0:T]
            for ci in range(2):
                self.mm(ps, self.m_pw[:, ci, co * 128:(co + 1) * 128], self.hs[:, ci, :], start=(ci == 0), stop=(ci == 1))
            self.ts(self.cat[:, 6 + co, :], ps, vec(71 + co), None, ALU.add, ALU.bypass)
        for dc in range(KC):
            py = self.pp()[:, 0:T]
            for c in range(8):
                self.mm(py, self.m_wout[:, c, dc * 128:(dc + 1) * 128], self.cat[:, c, :], start=(c == 0), stop=(c == 7))
            self.evac_with_stats(py, self.y_sb[:, dc, :], dc, KC)
        self.norm_out(self.y_sb, xin, l, 1, col)
        dst = dst_t[s, :, :, t0:t0 + T]
        P.dma("sync", dst.w(dst.ap.rearrange("k p t -> p k t")), xin[:, :, :], "xout%d" % slot)

    def mixer_sublayer(self, l, need_ctx):
        cfg = self.cfg
        self.load_mixer_weights(l)
        lr = self.lraug[:, :]
        self.P.op("vector", lambda e: e.memset(lr.ap, 1.0), writes=[lr])
        for s in range(cfg.nseq):
            for (n_tok, first) in ((cfg.nctx, True), (cfg.nlat, False)):
                pass
            self.gla_reset(1)
            self.gla_reset(0)
            if need_ctx:
                self.zero_pads(cfg.nctx)
            for t0 in reversed(range(0, cfg.nctx, T)):
                self.mixer_block_B(l, s, "ctx", t0, cfg.nseq, need_ctx)
            self.keep_D(1)
            for t0 in range(0, cfg.nctx, T):
                self.mixer_block_F(l, s, "ctx", t0, cfg.nseq, need_ctx, self.C)
            self.keep_D(0)
            self.zero_pads(cfg.nlat)
            for t0 in reversed(range(0, cfg.nlat, T)):
                self.mixer_block_B(l, s, "lat", t0, s, True)
            for t0 in range(0, cfg.nlat, T):
                self.mixer_block_F(l, s, "lat", t0, s, True, self.X)

    def zero_pads(self, n_tok):
        for (a, b) in ((0, 16), (16 + n_tok, 32 + n_tok)):
            v = self.HB[:, :, a:b]
            self.P.op("vector", lambda e, v=v: e.memset(v.ap, 0.0), writes=[v])


for _n, _f in list(MixerMixin.__dict__.items()):
    if callable(_f):
        setattr(Builder, _n, _f)


def build_full(cfg, upto=None):
    B = Builder(cfg)
    B.setup()
    B.mixer_consts()
    B.adaln()
    L = cfg.depth
    for l in range(L):
        last = l == L - 1
        first = l == 0
        stop = upto is not None and upto == (l, 0)
        B.ffn_sublayer(l, 0, 0, B.xT if first else B.X, B.outT if stop else B.X, do_ctx=True,
                       ctx_src=B.ctxT if first else B.C)
        if stop:
            break
        B.mixer_sublayer(l, need_ctx=not last)
        stop = upto is not None and upto == (l, 1)
        B.ffn_sublayer(l, 2, 1, B.X, B.outT if (last or stop) else B.X, do_ctx=not last, ctx_src=B.C)
        if stop:
            break
    if SCHED:
        B.P.schedule()
    B.P.emit()
    return B


def mixer_maps(inp, cfg):
    L = cfg.depth
    m = {}
    m["w_in"] = np.ascontiguousarray(
        inp["w_in"][:L].reshape(L, KC, 128, DIN).transpose(0, 2, 1, 3).reshape(L, 128, KC * DIN))
    m["w_out"] = np.ascontiguousarray(
        inp["w_out"][:L].reshape(L, KC, 128, D).transpose(0, 2, 1, 3).reshape(L, 128, KC * D))
    m["conv_pw"] = np.ascontiguousarray(
        inp["conv_pw"][:L].reshape(L, 2, 128, 256).transpose(0, 2, 1, 3).reshape(L, 128, 512))
    pw = np.zeros((L, 64, 4, 128), np.float32)
    for gi in range(4):
        o = (gi % 2) * 64
        pw[:, :, gi, o:o + 64] = inp["pool_w"][:L, gi]
    m["pool_w"] = pw.reshape(L, 64, 512)
    m["poolM"] = np.ascontiguousarray(POOLM.transpose(1, 0, 2).reshape(128, NPM * 128))
    ti = np.arange(128)
    same = (ti[:, None] // 64) == (ti[None, :] // 64)
    mf = (same & (ti[:, None] <= ti[None, :])).astype(np.float32)
    mb = (same & (ti[:, None] >= ti[None, :])).astype(np.float32)
    m["masks"] = np.ascontiguousarray(np.stack([mf, mb], axis=1).reshape(128, 256))
    m["cumM"] = np.ascontiguousarray(np.stack([mf, mb], axis=1) * np.float32(-1.0 / 16.0))
    wg = np.zeros((L, 32, 2, 256), np.float32)
    wg[:, 0:16] = inp["w_gk2"][:L].transpose(0, 2, 1, 3)
    wg[:, 16] = inp["b_gk"][:L]
    m["wgk"] = wg
    vec = np.zeros((128, L, NV), np.float32)
    for l in range(L):
        vec[:, l, 0] = inp["gla_norm_g"][l]
        for cc in range(2):
            sl = slice(cc * 128, (cc + 1) * 128)
            vec[:, l, 1 + cc] = inp["pool_scale"][l, sl]
            vec[:, l, 3 + cc * 31:3 + (cc + 1) * 31] = inp["conv_dw"][l][:, sl].T
            vec[:, l, 65 + cc] = inp["conv_dw_b"][l, sl]
            vec[:, l, 67 + cc] = inp["conv_ln_g"][l, sl]
            vec[:, l, 69 + cc] = inp["conv_ln_b"][l, sl]
            vec[:, l, 71 + cc] = inp["conv_pw_b"][l, sl]
    m["vecs"] = vec

    def cnt(n, w):
        lo, hi = w // 2, w - 1 - w // 2
        i = np.arange(n)
        return (np.minimum(i + hi + 1, n) - np.maximum(i - lo, 0)).astype(np.float32)
    rows = cfg.nlat // 64
    il = np.zeros((64, 4, cfg.nlat), np.float32)
    ic = np.zeros((64, 4, cfg.nctx), np.float32)
    for gi, w in enumerate(POOLW):
        c2 = (cnt(rows, w)[:, None] * cnt(64, w)[None, :]).reshape(-1)
        il[:, gi, :] = (np.float32(1.0) / c2)[None, :]
        ic[:, gi, :] = (np.float32(1.0) / cnt(cfg.nctx, w))[None, :]
    m["invc_lat"] = il
    m["invc_ctx"] = ic
    return m


_CACHE = {}


def kernel(**inputs):
    inp = {k: np.asarray(v) for k, v in inputs.items()}
    ncores = 8
    Bt, N, _ = inp["x"].shape
    cfg = Cfg(nseq=Bt // ncores, nlat=N, nctx=inp["ctx"].shape[1], depth=inp["w_ada"].shape[0])
    key = (cfg.nseq, cfg.nlat, cfg.nctx, cfg.depth)
    if key not in _CACHE:
        _CACHE[key] = build_full(cfg)
    B = _CACHE[key]
    shared = shared_weight_maps(inp, cfg.depth)
    shared.update(mixer_maps(inp, cfg))
    maps = [core_maps(inp, cfg, c, shared) for c in range(ncores)]
    res = run_bass_kernel_spmd(B.nc, maps, core_ids=list(range(ncores)))
    outs = []
    for c in range(ncores):
        oT = np.asarray(res.results[c]["outT"])
        outs.append(oT.reshape(cfg.nseq, D, N).transpose(0, 2, 1))
    return np.ascontiguousarray(np.concatenate(outs, axis=0)).astype(np.float32)
```

```python
import numpy as np
import concourse.bass as bass
import concourse.mybir as mybir
from concourse.bass_utils import run_bass_kernel_spmd

F32 = mybir.dt.float32
BF16 = mybir.dt.bfloat16
AF = mybir.ActivationFunctionType
ALU = mybir.AluOpType

ENGINES = ("tensor", "vector", "scalar", "gpsimd", "sync")
SEM_LIMIT = 2000


class V:
    __slots__ = ("ap", "key", "box")

    def __init__(self, ap, key, box):
        self.ap, self.key, self.box = ap, key, box

    def w(self, ap):
        return V(ap, self.key, self.box)


class TT:
    def __init__(self, name, handle, shape, is_dram=False):
        self.name, self.shape = name, tuple(int(s) for s in shape)
        self.base = handle.ap() if is_dram else handle[:]

    def __getitem__(self, idx):
        if not isinstance(idx, tuple):
            idx = (idx,)
        idx = tuple(idx) + (slice(None),) * (len(self.shape) - len(idx))
        box = []
        for i, n in zip(idx, self.shape):
            if isinstance(i, slice):
                lo = 0 if i.start is None else i.start
                hi = n if i.stop is None else i.stop
                assert i.step is None and 0 <= lo < hi <= n, (self.name, idx, self.shape)
                box.append((lo, hi))
            else:
                assert 0 <= i < n, (self.name, idx, self.shape)
                box.append((i, i + 1))
        return V(self.base[idx], self.name, tuple(box))


def _overlap(a, b):
    for (l0, h0), (l1, h1) in zip(a, b):
        if h0 <= l1 or h1 <= l0:
            return False
    return True


def _covers(a, b):
    for (l0, h0), (l1, h1) in zip(a, b):
        if l0 > l1 or h0 < h1:
            return False
    return True


class Op:
    __slots__ = ("eng", "fn", "deps", "signal", "dma_key", "ev", "waits", "id", "alld", "cost", "tag", "prio")


class Prog:
    def __init__(self, nc):
        self.nc = nc
        self.ops = []
        self.wr = {}
        self.rd = {}
        self.last_dma = {}
        self.psum_full = {}
        self.tensors = {}
        self._ctx = []

    def sbuf(self, name, shape, dtype):
        h = self.nc.alloc_sbuf_tensor(name, list(shape), dtype)
        t = TT(name, h, shape)
        self.tensors[name] = t
        return t

    def psum(self, name, shape, dtype):
        h = self.nc.alloc_psum_tensor(name, list(shape), dtype)
        t = TT(name, h, shape)
        self.tensors[name] = t
        self.psum_full[name] = tuple((0, int(n)) for n in shape)
        return t

    def dram(self, name, shape, dtype, kind="Internal"):
        h = self.nc.dram_tensor(name, list(shape), dtype, kind=kind)
        t = TT(name, h, shape, is_dram=True)
        self.tensors[name] = t
        return t

    def op(self, eng, fn, reads=(), writes=(), dma_key=None):
        o = Op()
        o.eng, o.fn, o.signal, o.dma_key, o.ev, o.waits = eng, fn, dma_key is not None, dma_key, None, None
        o.id = len(self.ops)
        o.tag = getattr(self, "cur_tag", "")
        o.prio = getattr(self, "cur_prio", 0)
        deps = {}
        if any(v.key in self.psum_full for v in reads) or any(v.key in self.psum_full for v in writes):
            reads = [V(v.ap, v.key, self.psum_full[v.key]) if v.key in self.psum_full else v for v in reads]
            writes = [V(v.ap, v.key, self.psum_full[v.key]) if v.key in self.psum_full else v for v in writes]
            writes = writes + [v for v in reads if v.key in self.psum_full]
        for r in reads:
            for box, pid in self.wr.get(r.key, ()):
                if _overlap(box, r.box):
                    deps[pid] = True
        for w in writes:
            for box, pid in self.wr.get(w.key, ()):
                if _overlap(box, w.box):
                    deps.setdefault(pid, False)
            for (box, _e), pid in self.rd.get(w.key, {}).items():
                if _overlap(box, w.box):
                    deps.setdefault(pid, False)
        if dma_key is not None and dma_key in self.last_dma:
            deps[self.last_dma[dma_key]] = True
        if dma_key is not None:
            self.last_dma[dma_key] = o.id
        deps.pop(o.id, None)
        o.alld = list(deps.keys())
        o.cost = self._cost(eng, reads, writes, dma_key)
        final = []
        for pid, raw in deps.items():
            p = self.ops[pid]
            if p.dma_key is None and dma_key is None and p.eng == eng:
                if eng == "tensor":
                    continue
            final.append(pid)
            p.signal = True
        o.deps = final
        for w in writes:
            lst = [(b, pid) for (b, pid) in self.wr.get(w.key, ()) if not _covers(w.box, b)]
            lst.append((w.box, o.id))
            self.wr[w.key] = lst
            rdd = self.rd.get(w.key)
            if rdd:
                for k in [k for k in rdd if _covers(w.box, k[0])]:
                    del rdd[k]
        for r in reads:
            self.rd.setdefault(r.key, {})[(r.box, eng if dma_key is None else "dma:" + dma_key)] = o.id
        self.ops.append(o)
        return o

    @staticmethod
    def _cost(eng, reads, writes, dma_key):
        def fsz(ap):
            n = 1
            for d in ap.shape[1:]:
                n *= int(d)
            return n
        if dma_key is not None:
            ap = writes[0].ap
            nbytes = fsz(ap) * int(ap.shape[0]) * (4 if ap.dtype == F32 else 2)
            return 2500.0 + nbytes / 200.0
        if eng == "tensor":
            rhs = reads[1].ap
            passes = 4 if rhs.dtype == F32 else 1
            return max(64, fsz(rhs)) * passes * 0.46 + 8.0
        n = fsz(writes[0].ap) if writes else 64
        if eng == "vector":
            return 130.0 + n * 1.15
        if eng == "scalar":
            return 240.0 + n * 0.95
        return 320.0 + n * 2.2

    def schedule(self, window=64):
        import heapq
        n = len(self.ops)
        ops = self.ops
        queues = {e: [o.id for o in ops if o.eng == e] for e in ENGINES}
        head = {e: 0 for e in ENGINES}
        placed = [False] * n
        finish = [0.0] * n
        pos = [0] * n
        for e in ENGINES:
            for k_, oid in enumerate(queues[e]):
                pos[oid] = k_
        hp = {e: [oid for oid in queues[e] if ops[oid].prio < 0] for e in ENGINES}
        hp_head = {e: 0 for e in ENGINES}
        tcur = {e: 0.0 for e in ENGINES}
        order = []
        remaining = n
        while remaining:
            best = None
            for e in ENGINES:
                q = queues[e]
                h = head[e]
                while h < len(q) and placed[q[h]]:
                    h += 1
                head[e] = h
                cnt = 0
                i = h
                hl = hp[e]
                j = hp_head[e]
                while j < len(hl) and placed[hl[j]]:
                    j += 1
                hp_head[e] = j
                got = False
                for jj in range(j, min(j + 12, len(hl))):
                    oid = hl[jj]
                    if placed[oid]:
                        continue
                    if pos[oid] - h > (900 if e == "tensor" else 250):
                        break
                    o = ops[oid]
                    ok = True
                    rdy = 0.0
                    for d in o.alld:
                        if not placed[d]:
                            ok = False
                            break
                        f = finish[d] + (60.0 if ops[d].eng == e and ops[d].dma_key is None else 350.0)
                        if f > rdy:
                            rdy = f
                    if ok and rdy <= tcur[e] + 30.0:
                        st = rdy if rdy > tcur[e] else tcur[e]
                        key = (st, -1)
                        if best is None or key < best[0]:
                            best = (key, e, oid)
                        got = True
                        break
                if got:
                    continue
                while i < len(q) and cnt < (window * 8 if e == "tensor" else window * 2):
                    oid = q[i]
                    i += 1
                    if placed[oid]:
                        continue
                    cnt += 1
                    o = ops[oid]
                    ok = True
                    rdy = 0.0
                    for d in o.alld:
                        if not placed[d]:
                            ok = False
                            break
                        f = finish[d] + (60.0 if ops[d].eng == e and ops[d].dma_key is None else 350.0)
                        if f > rdy:
                            rdy = f
                    if not ok:
                        continue
                    st = rdy if rdy > tcur[e] else tcur[e]
                    key = (st, oid)
                    if best is None or key < best[0]:
                        best = (key, e, oid)
                    if rdy <= tcur[e]:
                        break
            assert best is not None, "scheduler stuck"
            (st, _), e, oid = best
            o = ops[oid]
            placed[oid] = True
            if o.dma_key is not None:
                tcur[e] = st + 60.0
                finish[oid] = st + o.cost
            else:
                tcur[e] = st + o.cost
                finish[oid] = tcur[e]
            order.append(oid)
            remaining -= 1
        self.sched_order = order
        self.est_ns = max(finish) if n else 0.0
        self.finish_t = finish

    def dma(self, q, out, in_, key):
        return self.op(q, lambda e, o=out.ap, i=in_.ap: e.dma_start(out=o, in_=i),
                       reads=[in_], writes=[out], dma_key=key)

    def emit(self):
        nc = self.nc
        sems = {}
        cnt = {}
        known = {e: {} for e in ENGINES}
        snaps = {}
        nsem = [0]

        def new_sem(base):
            nsem[0] += 1
            k = "%s_%d" % (base, nsem[0])
            sems[k] = nc.alloc_semaphore(k)
            return k

        seq = [self.ops[i] for i in self.sched_order] if getattr(self, "sched_order", None) else self.ops
        for o in seq:
            kn = known[o.eng]
            waits = []
            for pid in sorted(o.deps, reverse=True):
                p = self.ops[pid]
                sk, val = p.ev
                if kn.get(sk, 0) >= val:
                    continue
                waits.append((sk, val))
                kn[sk] = val
                for k2, v2 in snaps[pid].items():
                    if kn.get(k2, 0) < v2:
                        kn[k2] = v2
            o.waits = waits
            if o.signal:
                cname = ("dma:" + o.dma_key) if o.dma_key is not None else o.eng
                inc = 16 if o.dma_key is not None else 1
                sk, v = cnt.get(cname, (None, 0))
                if sk is None or v + inc > SEM_LIMIT:
                    sk, v = new_sem(cname.replace(":", "_")), 0
                v += inc
                cnt[cname] = (sk, v)
                o.ev = (sk, v)
                snaps[o.id] = dict(kn)
        self.n_sems = nsem[0]
        self.sem_final = dict(cnt)
        per_eng = {e: [o for o in seq if o.eng == e] for e in ENGINES}
        final_waits = [(sk, v) for cname, (sk, v) in cnt.items() if cname.startswith("dma:")]

        def run(e, eng):
            for o in per_eng[e]:
                for sk, val in o.waits:
                    eng.wait_ge(sems[sk], val)
                ins = o.fn(eng)
                if o.signal:
                    ins.then_inc(sems[o.ev[0]], 16 if o.dma_key is not None else 1)
            if e == "sync":
                for sk, v in final_waits:
                    eng.wait_ge(sems[sk], v)

        with nc.Block() as block:
            @block.tensor
            def _(eng):
                run("tensor", eng)

            @block.vector
            def _(eng):
                run("vector", eng)

            @block.scalar
            def _(eng):
                run("scalar", eng)

            @block.gpsimd
            def _(eng):
                run("gpsimd", eng)

            @block.sync
            def _(eng):
                run("sync", eng)


class SubT:
    def __init__(self, parent, off, shape, f32=False, p0=0):
        self.parent, self.off, self.shape = parent, off, tuple(shape)
        self.mul = 2 if f32 else 1
        self.p0 = p0
        n = 1
        for s_ in shape[1:]:
            n *= s_
        self.size = n * self.mul
        strides = []
        acc = 1
        for s_ in reversed(shape[1:]):
            strides.append(acc)
            acc *= s_
        self.strides = tuple(reversed(strides))
        ap = parent.base[p0:p0 + shape[0], off:off + self.size]
        if f32:
            ap = ap.bitcast(F32)
        if len(shape) == 3:
            ap = ap.rearrange("p (a b) -> p a b", a=shape[1])
        elif len(shape) == 4:
            ap = ap.rearrange("p (a b c) -> p a b c", a=shape[1], b=shape[2])
        self.base = ap

    def __getitem__(self, idx):
        if not isinstance(idx, tuple):
            idx = (idx,)
        idx = tuple(idx) + (slice(None),) * (len(self.shape) - len(idx))
        rng = []
        for i, n in zip(idx, self.shape):
            if isinstance(i, slice):
                lo = 0 if i.start is None else i.start
                hi = n if i.stop is None else i.stop
                assert 0 <= lo < hi <= n, (idx, self.shape)
                rng.append((lo, hi))
            else:
                assert 0 <= i < n, (idx, self.shape)
                rng.append((i, i + 1))
        flo = self.off + self.mul * sum(l * s_ for (l, _h), s_ in zip(rng[1:], self.strides))
        fhi = self.off + self.mul * (sum((h - 1) * s_ for (_l, h), s_ in zip(rng[1:], self.strides)) + 1)
        return V(self.base[idx], self.parent.name, ((self.p0 + rng[0][0], self.p0 + rng[0][1]), (flo, fhi)))


D = 1024
KC = 8
DFF = 2816
HC = 22
NMOD = 9
T = 256
A2N = 12544
EPS = 1e-6
DIN = 2336


class Cfg:
    def __init__(self, nseq=2, nlat=4096, nctx=256, depth=2, stages=None):
        self.nseq, self.nlat, self.nctx, self.depth = nseq, nlat, nctx, depth
        self.stages = stages


class Builder:
    def __init__(self, cfg):
        self.cfg = cfg
        nc = bass.Bass("TRN2", target_bir_lowering=False)
        self.nc = nc
        P = Prog(nc)
        self.P = P
        L, S = cfg.depth, cfg.nseq
        self.ncol = S + 1
        NCOL = 4
        self.xT = P.dram("xT", [S, KC, 128, cfg.nlat], F32, kind="ExternalInput")
        self.ctxT = P.dram("ctxT", [S, KC, 128, cfg.nctx], F32, kind="ExternalInput")
        self.cT = P.dram("cT", [128, KC, NCOL], F32, kind="ExternalInput")
        self.w_ada = P.dram("w_ada", [L, 36, 128, KC, 256], F32, kind="ExternalInput")
        self.b_ada = P.dram("b_ada", [128, L, 72], F32, kind="ExternalInput")
        self.norm_g = P.dram("norm_g", [128, L, 6, KC], F32, kind="ExternalInput")
        self.ffn_up = P.dram("ffn_up", [L, 2, 128, KC * 2 * DFF], F32, kind="ExternalInput")
        self.ffn_down = P.dram("ffn_down", [L, 2, 128, HC * D], F32, kind="ExternalInput")
        self.outT = P.dram("outT", [S, KC, 128, cfg.nlat], F32, kind="ExternalOutput")
        self.X = P.dram("Xs", [S, KC, 128, cfg.nlat], F32)
        self.C = P.dram("Cs", [S, KC, 128, cfg.nctx], F32)
        self.declare_mixer_io()
        self.W = P.sbuf("W", [128, KC * 2 * DFF + HC * D], BF16)
        self.w_up = SubT(self.W, 0, (128, KC, 2 * DFF))
        self.w_down = SubT(self.W, KC * 2 * DFF, (128, HC, D))
        self.xin = [P.sbuf("xin%d" % i, [128, KC, T], F32) for i in range(2)]
        self.hin = [P.sbuf("hin%d" % i, [128, KC, T], BF16) for i in range(2)]
        self.scr = [P.sbuf("scr%d" % i, [128, T], F32) for i in range(3)]
        self.rs = [P.sbuf("rs%d" % i, [128, T], F32) for i in range(2)]
        self.A2 = P.sbuf("A2", [128, A2N], BF16)
        self.s_sb = [SubT(self.A2, i * HC * T, (128, HC, T)) for i in range(2)]
        o_ = 2 * HC * T
        self.sqA = [SubT(self.A2, o_ + i * T, (128, T)) for i in range(2)]
        self.sqC = [SubT(self.A2, o_ + (2 + i) * T, (128, T)) for i in range(2)]
        self.saB = SubT(self.A2, o_ + 4 * T, (128, T))
        assert o_ + 5 * T <= A2N
        self.ffn_mode = False
        self.y_sb = P.sbuf("y_sb", [128, KC, T], F32)
        self.ones = P.sbuf("ones", [128, 128], F32)
        self.ones_bf = P.sbuf("ones_bf", [128, 128], BF16)
        self.sc = P.sbuf("sc", [128, KC, NCOL], F32)
        self.mods = P.sbuf("mods", [128, L, 72, NCOL], F32)
        self.badd = P.sbuf("badd", [128, L, 72], F32)
        self.gn = P.sbuf("gn", [128, L, 6, KC], F32)
        self.modA = P.sbuf("modA", [128, L, 3, KC, NCOL], F32)
        self.modG = P.sbuf("modG", [128, L, 3, KC, NCOL], F32)
        self.pb = [P.psum("pb%d" % i, [128, 512], F32) for i in range(8)]
        self.ps_stat = self.pb[0]
        self.ps_a = [self.pb[1], self.pb[2]]
        self.ps_b = [self.pb[3], self.pb[4]]
        self.ps_y = [self.pb[5], self.pb[6]]
        self.ps_m = self.pb[7]
        self.alloc_mixer()
        self.cnt = {}

    def scr_bf(self, kind=None):
        if self.ffn_mode and kind == "A":
            return self.sqA[self.rot("sqA", 2)][:, :]
        if self.ffn_mode and kind == "C":
            return self.sqC[self.rot("sqC", 2)][:, :]
        t = self.scr[self.rot("scr", 3)][:, :]
        return t.w(t.ap.bitcast(BF16)[:, 0:T])

    def scr_f(self, kind=None):
        if self.ffn_mode and kind == "A":
            return self.scr[self.rot("tmpA", 2)][:, :]
        if self.ffn_mode and kind == "C":
            return self.scr[2][:, :]
        return self.scr[self.rot("scr", 3)][:, :]

    def rot(self, name, n):
        v = self.cnt.get(name, 0)
        self.cnt[name] = v + 1
        return v % n

    def setup(self):
        P = self.P
        ones = self.ones[:, :]
        P.op("vector", lambda e: e.memset(ones.ap, 1.0), writes=[ones])
        onesb = self.ones_bf[:, :]
        P.op("vector", lambda e: e.memset(onesb.ap, 1.0), writes=[onesb])
        P.dma("sync", self.sc[:, :, :], self.cT[:, :, :], "small")
        P.dma("sync", self.badd[:, :, :], self.b_ada[:, :, :], "small")
        P.dma("sync", self.gn[:, :, :, :], self.norm_g[:, :, :, :], "small")
        sc = self.sc[:, :, :]
        P.op("scalar", lambda e: e.activation(out=sc.ap, in_=sc.ap, func=AF.Silu), reads=[sc], writes=[sc])

    def adaln(self):
        P = self.P
        L = self.cfg.depth
        for l in range(L):
            for u in range(36):
                slot = self.rot("xin", 2)
                st = self.xin[slot]
                P.dma("sync", st[:, :, :], self.w_ada[l, u, :, :, :], "xin%d" % slot)
                for j in range(2):
                    col = (u * 2 + j) * 4
                    out = self.ps_m[:, col:col + 4]
                    for kc in range(KC):
                        lhs = st[:, kc, j * 128:(j + 1) * 128]
                        rhs = self.sc[:, kc, :]
                        P.op("tensor",
                             lambda e, o=out.ap, a=lhs.ap, b=rhs.ap, k=kc: e.matmul(
                                 o, lhsT=a, rhs=b, start=(k == 0), stop=(k == KC - 1)),
                             reads=[lhs, rhs], writes=[out])
            psv = self.ps_m[:, 0:288]
            mo = self.mods[:, l, :, :]
            bb = self.badd[:, l, :]
            P.op("vector",
                 lambda e, o=mo.ap, p=psv.ap, b=bb.ap: e.tensor_tensor(
                     out=o, in0=p.rearrange("p (a b) -> p a b", b=4),
                     in1=b.unsqueeze(2).broadcast_to([128, 72, 4]), op=ALU.add),
                 reads=[psv, bb], writes=[mo])
            for i in range(3):
                scale = self.mods[:, l, (3 * i + 1) * KC:(3 * i + 2) * KC, :]
                gate = self.mods[:, l, (3 * i + 2) * KC:(3 * i + 3) * KC, :]
                g_in = self.gn[:, l, 2 * i, :]
                g_out = self.gn[:, l, 2 * i + 1, :]
                A = self.modA[:, l, i, :, :]
                G = self.modG[:, l, i, :, :]
                wgt = 1.0 if i == 1 else 0.5
                P.op("vector",
                     lambda e, o=A.ap, s=scale.ap, g=g_in.ap: e.scalar_tensor_tensor(
                         out=o, in0=s, scalar=1.0, in1=g.unsqueeze(2).broadcast_to([128, KC, 4]),
                         op0=ALU.add, op1=ALU.mult),
                     reads=[scale, g_in], writes=[A])
                P.op("vector",
                     lambda e, o=G.ap, s=gate.ap, g=g_out.ap, w=wgt: e.scalar_tensor_tensor(
                         out=o, in0=s, scalar=w, in1=g.unsqueeze(2).broadcast_to([128, KC, 4]),
                         op0=ALU.mult, op1=ALU.mult),
                     reads=[gate, g_out], writes=[G])

    def shift(self, l, i, kc, col):
        return self.mods[:, l, 3 * i * KC + kc, col:col + 1]

    def load_ffn_weights(self, l, which):
        P = self.P
        CH = 2048
        n_up = KC * 2 * DFF
        for c0 in range(0, n_up, CH):
            c1 = min(n_up, c0 + CH)
            dst = V(self.W.base[:, c0:c1], "W", ((0, 128), (c0, c1)))
            P.dma("gpsimd", dst, self.ffn_up[l, which, :, c0:c1], "wload%d" % self.rot("wl", 4))
        n_dn = HC * D
        for c0 in range(0, n_dn, CH):
            c1 = min(n_dn, c0 + CH)
            dst = V(self.W.base[:, n_up + c0:n_up + c1], "W", ((0, 128), (n_up + c0, n_up + c1)))
            P.dma("gpsimd", dst, self.ffn_down[l, which, :, c0:c1], "wload%d" % self.rot("wl", 4))

    def rstd_from_stat(self, ps, rs):
        P = self.P
        P.op("scalar", lambda e, o=rs.ap, i=ps.ap: e.activation(out=o, in_=i, func=AF.Ln, bias=EPS, scale=1.0 / D),
             reads=[ps], writes=[rs])
        P.op("scalar", lambda e, o=rs.ap: e.activation(out=o, in_=o, func=AF.Exp, scale=-0.5),
             reads=[rs], writes=[rs])

    def norm_in(self, xin, l, i, col, hin):
        P = self.P
        st = self.ps_stat[:, 0:T]
        for kc in range(KC):
            sq = self.scr_bf("A")
            xk = xin[:, kc, :]
            P.op("scalar", lambda e, o=sq.ap, i_=xk.ap: e.activation(out=o, in_=i_, func=AF.Square),
                 reads=[xk], writes=[sq])
            P.op("tensor", lambda e, o=st.ap, a=self.ones_bf[:, :].ap, b=sq.ap, k=kc: e.matmul(
                o, lhsT=a, rhs=b, start=(k == 0), stop=(k == KC - 1)),
                reads=[self.ones_bf[:, :], sq], writes=[st])
        rs = self.rs[self.rot("rs", 2)][:, :]
        self.rstd_from_stat(st, rs)
        for kc in range(KC):
            tmp = self.scr_f("A")
            xk = xin[:, kc, :]
            P.op("vector", lambda e, o=tmp.ap, a=xk.ap, b=rs.ap: e.tensor_tensor(out=o, in0=a, in1=b, op=ALU.mult),
                 reads=[xk, rs], writes=[tmp])
            A = self.modA[:, l, i, kc, col:col + 1]
            sh = self.shift(l, i, kc, col)
            hk = hin[:, kc, :]
            if kc % 2 == 0:
                P.op("gpsimd", lambda e, o=hk.ap, t=tmp.ap, a=A.ap, s=sh.ap: e.tensor_scalar(
                    out=o, in0=t, scalar1=a, scalar2=s, op0=ALU.mult, op1=ALU.add),
                    reads=[tmp, A, sh], writes=[hk])
            else:
                P.op("scalar", lambda e, o=hk.ap, t=tmp.ap, a=A.ap, s=sh.ap: e.activation(
                    out=o, in_=t, func=AF.Identity, bias=s, scale=a),
                    reads=[tmp, A, sh], writes=[hk])

    def norm_out(self, y_sb, xin, l, i, col, st=None):
        P = self.P
        st = self.ps_stat[:, 0:T] if st is None else st
        rs = self.rs[self.rot("rs", 2)][:, :]
        self.rstd_from_stat(st, rs)
        for kc in range(KC):
            tmp = self.scr_f("C")
            yk = y_sb[:, kc, :]
            xk = xin[:, kc, :]
            P.op("vector", lambda e, o=tmp.ap, a=yk.ap, b=rs.ap: e.tensor_tensor(out=o, in0=a, in1=b, op=ALU.mult),
                 reads=[yk, rs], writes=[tmp])
            G = self.modG[:, l, i, kc, col:col + 1]
            P.op("vector", lambda e, o=xk.ap, t=tmp.ap, g=G.ap: e.scalar_tensor_tensor(
                out=o, in0=t, scalar=g, in1=o, op0=ALU.mult, op1=ALU.add),
                reads=[tmp, G, xk], writes=[xk])

    def evac_with_stats(self, ps, dst, k, n, st=None):
        P = self.P
        st = self.ps_stat[:, 0:T] if st is None else st
        P.op("vector", lambda e, o=dst.ap, i_=ps.ap: e.tensor_copy(out=o, in_=i_), reads=[ps], writes=[dst])
        sq = self.scr_bf("C")
        P.op("scalar", lambda e, o=sq.ap, i_=dst.ap: e.activation(out=o, in_=i_, func=AF.Square),
             reads=[dst], writes=[sq])
        P.op("tensor", lambda e, o=st.ap, a=self.ones_bf[:, :].ap, b=sq.ap: e.matmul(
            o, lhsT=a, rhs=b, start=(k == 0), stop=(k == n - 1)),
            reads=[self.ones_bf[:, :], sq], writes=[st])

    def ffn_block(self, l, i, col, src, dst):
        P = self.P
        slot = self.rot("xin", 2)
        xin = self.xin[slot]
        bid = self.rot("blk", 1 << 30)
        P.cur_tag = "b%d:A" % bid
        P.cur_prio = -1
        P.dma("sync", xin[:, :, :], src.w(src.ap.rearrange("k p t -> p k t")), "xin%d" % slot)
        hin = self.hin[self.rot("hin", 2)]
        self.norm_in(xin, l, i, col, hin)
        P.cur_prio = 0
        P.cur_tag = "b%d:B" % bid
        s_sb = self.s_sb[self.rot("s", 2)]
        dbg = getattr(self, "dbg", "all")
        for hc in range(HC if dbg != "up1" else 1):
            pab = self.pb[1 + self.rot("ps_ab", 4)]
            pa = pab[:, 0:T]
            pb = pab[:, T:2 * T]
            for (ps, off) in ((pa, 0), (pb, DFF)):
                for kc in range(KC):
                    lhs = self.w_up[:, kc, off + hc * 128: off + (hc + 1) * 128]
                    rhs = hin[:, kc, :]
                    P.op("tensor", lambda e, o=ps.ap, a=lhs.ap, b=rhs.ap, k=kc: e.matmul(
                        o, lhsT=a, rhs=b, start=(k == 0), stop=(k == KC - 1)),
                        reads=[lhs, rhs], writes=[ps])
            sa = self.saB[:, :]
            P.op("scalar", lambda e, o=sa.ap, i_=pa.ap: e.activation(out=o, in_=i_, func=AF.Silu),
                 reads=[pa], writes=[sa])
            sk = s_sb[:, hc, :]
            P.op("vector", lambda e, o=sk.ap, a=sa.ap, b=pb.ap: e.tensor_tensor(out=o, in0=a, in1=b, op=ALU.mult),
                 reads=[sa, pb], writes=[sk])
        P.cur_tag = "b%d:C" % bid
        for dc in range(KC if dbg in ("all", "down") else 0):
            py = self.pb[5 + self.rot("ps_y", 2)][:, 0:T]
            for hc in range(HC):
                lhs = self.w_down[:, hc, dc * 128:(dc + 1) * 128]
                rhs = s_sb[:, hc, :]
                P.op("tensor", lambda e, o=py.ap, a=lhs.ap, b=rhs.ap, k=hc: e.matmul(
                    o, lhsT=a, rhs=b, start=(k == 0), stop=(k == HC - 1)),
                    reads=[lhs, rhs], writes=[py])
            self.evac_with_stats(py, self.y_sb[:, dc, :], dc, KC, st=self.pb[7][:, 0:T])
        if dbg == "all":
            self.norm_out(self.y_sb, xin, l, i, col, st=self.pb[7][:, 0:T])
        P.dma("sync", dst.w(dst.ap.rearrange("k p t -> p k t")), xin[:, :, :], "xout%d" % slot)

    def ffn_sublayer(self, l, i, which, lat_src, lat_dst, do_ctx=True, ctx_src=None):
        cfg = self.cfg
        self.load_ffn_weights(l, which)
        self.ffn_mode = True
        for s in range(cfg.nseq):
            if do_ctx:
                cs = (ctx_src if ctx_src is not None else self.C)
                self.ffn_block(l, i, cfg.nseq, cs[s, :, :, 0:T], self.C[s, :, :, 0:T])
            for t0 in range(0, cfg.nlat, T):
                self.ffn_block(l, i, s, lat_src[s, :, :, t0:t0 + T], lat_dst[s, :, :, t0:t0 + T])
        self.ffn_mode = False


def _fm(a):
    B, N, Dm = a.shape
    return np.ascontiguousarray(a.transpose(0, 2, 1).reshape(B, Dm // 128, 128, N))


def shared_weight_maps(inp, L):
    m = {}
    wa = inp["w_ada"][:L]
    m["w_ada"] = np.ascontiguousarray(
        wa.reshape(L, KC, 128, 36, 256).transpose(0, 3, 2, 1, 4))
    m["b_ada"] = np.ascontiguousarray(inp["b_ada"][:L].reshape(L, 72, 128).transpose(2, 0, 1))
    m["norm_g"] = np.ascontiguousarray(inp["norm_g"][:L].reshape(L, 6, KC, 128).transpose(3, 0, 1, 2))
    ups, dns = [], []
    for l in range(L):
        u, d = [], []
        for nm_u, nm_d in (("ffn1_up", "ffn1_down"), ("ffn2_up", "ffn2_down")):
            wu = inp[nm_u][l]
            u.append(wu.reshape(KC, 128, 2 * DFF).transpose(1, 0, 2).reshape(128, KC * 2 * DFF))
            wd = inp[nm_d][l]
            d.append(wd.reshape(HC, 128, D).transpose(1, 0, 2).reshape(128, HC * D))
        ups.append(np.stack(u))
        dns.append(np.stack(d))
    m["ffn_up"] = np.ascontiguousarray(np.stack(ups))
    m["ffn_down"] = np.ascontiguousarray(np.stack(dns))
    return m


def core_maps(inp, cfg, core, shared):
    S = cfg.nseq
    b0 = core * S
    m = dict(shared)
    m["xT"] = _fm(inp["x"][b0:b0 + S])
    m["ctxT"] = _fm(inp["ctx"][b0:b0 + S])
    cols = [inp["c"][b0 + s] for s in range(S)] + [inp["c_ctx"]]
    while len(cols) < 4:
        cols.append(np.zeros_like(inp["c_ctx"]))
    cT = np.stack(cols, axis=1)
    m["cT"] = np.ascontiguousarray(cT.reshape(KC, 128, 4).transpose(1, 0, 2))
    return m


NV = 80
SCHED = True
POOLW = (2, 4, 8, 16)
LN8 = float(np.log(0.125))


def pool_tables():
    mats, idx = [], {}
    ti = np.arange(128)
    for gi, w in enumerate(POOLW):
        lo, hi = w // 2, w - 1 - w // 2
        for dl in range(-5, 6):
            ri, ci = ti // 64, ti % 64
            dr = 2 * dl + ri[:, None] - ri[None, :]
            dc = ci[:, None] - ci[None, :]
            m = ((dr >= -lo) & (dr <= hi) & (dc >= -lo) & (dc <= hi)).astype(np.float32)
            if m.any():
                idx[("lat", gi, dl)] = len(mats)
                mats.append(m)
        for dl in (-1, 0, 1):
            dt_ = 128 * dl + ti[:, None] - ti[None, :]
            m = ((dt_ >= -lo) & (dt_ <= hi)).astype(np.float32)
            if m.any():
                idx[("ctx", gi, dl)] = len(mats)
                mats.append(m)
    return np.stack(mats), idx


POOLM, POOLIDX = pool_tables()
NPM = POOLM.shape[0]


def _bm(self):
    pass


def declare_mixer_io(self):
    P, cfg = self.P, self.cfg
    L, S = cfg.depth, cfg.nseq
    self.w_in_d = P.dram("w_in", [L, 128, KC * DIN], F32, kind="ExternalInput")
    self.w_out_d = P.dram("w_out", [L, 128, KC * D], F32, kind="ExternalInput")
    self.pw_d = P.dram("conv_pw", [L, 128, 512], F32, kind="ExternalInput")
    self.poolw_d = P.dram("pool_w", [L, 64, 512], F32, kind="ExternalInput")
    self.poolM_d = P.dram("poolM", [128, NPM * 128], F32, kind="ExternalInput")
    self.masks_d = P.dram("masks", [128, 256], F32, kind="ExternalInput")
    self.cumM_d = P.dram("cumM", [128, 2, 128], F32, kind="ExternalInput")
    self.wgk_d = P.dram("wgk", [L, 32, 2, 256], F32, kind="ExternalInput")
    self.vecs_d = P.dram("vecs", [128, L, NV], F32, kind="ExternalInput")
    self.invl_d = P.dram("invc_lat", [64, 4, cfg.nlat], F32, kind="ExternalInput")
    self.invc_d = P.dram("invc_ctx", [64, 4, cfg.nctx], F32, kind="ExternalInput")
    self.OBl = P.dram("OBl", [S, 128, 4, cfg.nlat], F32)
    self.OBc = P.dram("OBc", [S, 128, 4, cfg.nctx], F32)


def alloc_mixer(self):
    P, cfg = self.P, self.cfg
    L = cfg.depth
    W = self.W
    o = [0]

    def carve(shape, f32=False, arena=None, p0=0):
        ar = arena or W
        key = id(ar)
        off = self._off.setdefault(key, 0)
        t = SubT(ar, off, shape, f32=f32, p0=p0)
        self._off[key] = off + t.size
        return t
    self._off = {}
    self.m_win = carve((128, KC, DIN))
    self.m_wout = carve((128, KC, D))
    self.m_pw = carve((128, 2, 256))
    self.m_poolw = carve((64, 4, 128))
    self.m_poolM = carve((128, NPM, 128))
    self.m_mask = carve((128, 2, 128))
    nsub = cfg.nlat // 128
    self.UT = carve((128, nsub, 256))
    self.HB = carve((128, 2, cfg.nlat + 32), f32=True)
    self.qT = carve((64, 4, T), f32=True)
    self.kT = carve((64, 4, T), f32=True)
    self.gs = carve((128, 4, T), f32=True)
    self.cat = carve((128, 8, T))
    self.uT = carve((64, 4, T), f32=True)
    assert self._off[id(W)] <= KC * 2 * DFF + HC * D, self._off[id(W)]
    A2 = self.A2
    self.ktok = carve((128, 2, 256), f32=True, arena=A2)
    self.vtok = carve((128, 2, 512), arena=A2)
    self.lraug = carve((32, T), arena=A2)
    self.sp = carve((128, 256), arena=A2)
    self.EqT = carve((64, 4, 128), f32=True, arena=A2)
    self.EkT = carve((64, 4, 128), f32=True, arena=A2)
    self.Ektok = carve((128, 256), f32=True, arena=A2)
    self.qt = carve((64, 4, 128), arena=A2)
    self.kt = carve((64, 4, 128), arena=A2)
    self.kttok = carve((128, 256), arena=A2)
    self.attm = carve((128, 4, 128), arena=A2)
    self.hs = SubT(A2, self.attm.off, (128, 2, T))
    self.Tst = carve((64, 4, 128), f32=True, arena=A2)
    self.S32 = carve((64, 4, 128), f32=True, arena=A2)
    self.Sbf = carve((64, 4, 128), arena=A2)
    self.o_sb = carve((128, 4, 128), f32=True, arena=A2)
    self.ob_sb = carve((128, 4, 128), f32=True, arena=A2)
    self.invt = SubT(A2, self.ob_sb.off, (64, 4, 128), f32=True)
    self.dpl = carve((64, 4, 128), arena=A2)
    assert self._off[id(A2)] <= A2N, self._off[id(A2)]
    self.Dbuf = P.sbuf("Dbuf", [64, 4, 4, 2], F32)
    self.Dsave = P.sbuf("Dsave", [64, 2, 4, 1], F32)
    self.Tdir = {0: self.Tst, 1: P.sbuf("Tst_b", [64, 4, 128], F32)}
    self.Dprev = {}
    self.onesc = P.sbuf("onesc", [64, 4, 1], F32)
    self.cumM = P.sbuf("cumM_sb", [128, 2, 128], BF16)
    self.wgk = P.sbuf("wgk_sb", [32, L, 2, 256], BF16)
    self.vecs = P.sbuf("vecs_sb", [128, L, NV], F32)


Builder.declare_mixer_io = declare_mixer_io
Builder.alloc_mixer = alloc_mixer


class MixerMixin:
    def mm(self, out, lhsT, rhs, start=True, stop=True, skip=False):
        self.P.op("tensor", lambda e, o=out.ap, a=lhsT.ap, b=rhs.ap: e.matmul(
            o, lhsT=a, rhs=b, start=start, stop=stop, skip_group_check=skip),
            reads=[lhsT, rhs], writes=[out])

    def act(self, out, in_, func, bias=None, scale=None, extra_reads=()):
        kw = {}
        if bias is not None:
            kw["bias"] = bias.ap if isinstance(bias, V) else bias
        if scale is not None:
            kw["scale"] = scale.ap if isinstance(scale, V) else scale
        rd = [in_] + [x for x in (bias, scale) if isinstance(x, V)] + list(extra_reads)
        self.P.op("scalar", lambda e, o=out.ap, i=in_.ap: e.activation(out=o, in_=i, func=func, **kw),
                  reads=rd, writes=[out])

    def tt(self, out, a, b, op, eng="vector", bap=None):
        b_ap = b.ap if bap is None else bap
        self.P.op(eng, lambda e, o=out.ap, x=a.ap, y=b_ap: e.tensor_tensor(out=o, in0=x, in1=y, op=op),
                  reads=[a, b], writes=[out])

    def ts(self, out, a, s1, s2, op0, op1, eng="vector"):
        rd = [a] + [x for x in (s1, s2) if isinstance(x, V)]
        g = lambda x: x.ap if isinstance(x, V) else x
        self.P.op(eng, lambda e, o=out.ap, x=a.ap: e.tensor_scalar(
            out=o, in0=x, scalar1=g(s1), scalar2=g(s2), op0=op0, op1=op1), reads=rd, writes=[out])

    def stt(self, out, a, sc, b, op0, op1, bap=None):
        rd = [a, b] + ([sc] if isinstance(sc, V) else [])
        g = lambda x: x.ap if isinstance(x, V) else x
        b_ap = b.ap if bap is None else bap
        self.P.op("vector", lambda e, o=out.ap, x=a.ap, y=b_ap: e.scalar_tensor_tensor(
            out=o, in0=x, scalar=g(sc), in1=y, op0=op0, op1=op1), reads=rd, writes=[out])

    def cp(self, out, in_, eng=None):
        if eng is None:
            eng = ("scalar", "vector", "scalar")[self.rot("cp", 3)]
        if eng == "scalar":
            self.P.op("scalar", lambda e, o=out.ap, i=in_.ap: e.activation(out=o, in_=i, func=AF.Copy),
                      reads=[in_], writes=[out])
        else:
            self.P.op(eng, lambda e, o=out.ap, i=in_.ap: e.tensor_copy(out=o, in_=i), reads=[in_], writes=[out])

    def cacc(self, cc):
        return self.y_sb[:, 2 + cc, :]

    def pp(self):
        return self.pb[1 + self.rot("pp", 2)]

    def mixer_consts(self):
        P = self.P
        P.dma("gpsimd", self.cumM[:, :, :], self.cumM_d[:, :, :], "wload%d" % self.rot("wl", 4))
        P.dma("gpsimd", self.wgk[:, :, :, :], self.wgk_d[:, :, :, :].w(self.wgk_d.base.rearrange("l p d c -> p l d c")), "wload%d" % self.rot("wl", 4))
        P.dma("sync", self.vecs[:, :, :], self.vecs_d[:, :, :], "small")
        oc = self.onesc[:, :, :]
        P.op("vector", lambda e: e.memset(oc.ap, 1.0), writes=[oc])

    def load_mixer_weights(self, l):
        P = self.P
        CH = 2048

        def ld(dst_t, src_t, src_idx, n, p=128):
            for c0 in range(0, n, CH):
                c1 = min(n, c0 + CH)
                dst = V(self.W.base[0:p, dst_t.off + c0:dst_t.off + c1], "W", ((0, p), (dst_t.off + c0, dst_t.off + c1)))
                P.dma("gpsimd", dst, src_t[src_idx + (slice(c0, c1),)], "wload%d" % self.rot("wl", 4))
        ld(self.m_win, self.w_in_d, (l, slice(None)), KC * DIN)
        ld(self.m_wout, self.w_out_d, (l, slice(None)), KC * D)
        ld(self.m_pw, self.pw_d, (l, slice(None)), 512)
        ld(self.m_poolw, self.poolw_d, (l, slice(None)), 512, p=64)
        ld(self.m_poolM, self.poolM_d, (slice(None),), NPM * 128)
        ld(self.m_mask, self.masks_d, (slice(None),), 256)

    def proj_fm(self, hin, col0, M, out):
        for kc in range(KC):
            self.mm(out, self.m_win[:, kc, col0:col0 + M], hin[:, kc, :], start=(kc == 0), stop=(kc == KC - 1))

    def proj_tok(self, hin, sub, col0, n, out):
        for kc in range(KC):
            self.mm(out, hin[:, kc, sub * 128:(sub + 1) * 128], self.m_win[:, kc, col0:col0 + n],
                    start=(kc == 0), stop=(kc == KC - 1))

    def gla_proj(self, hin, d):
        for h in range(4):
            ps = self.pp()[0:64, 0:T]
            self.proj_fm(hin, h * 64, 64, ps)
            self.cp(self.qT[:, h, :], ps)
            ps = self.pp()[0:64, 0:T]
            self.proj_fm(hin, 256 + h * 64, 64, ps)
            self.cp(self.kT[:, h, :], ps)
        ps = self.pp()[0:16, 0:T]
        self.proj_fm(hin, 1536 + 16 * d, 16, ps)
        self.cp(self.lraug[0:16, :], ps)
        for sub in range(2):
            ps = self.pp()[:, 0:256]
            self.proj_tok(hin, sub, 256, 256, ps)
            self.cp(self.ktok[:, sub, :], ps)
            ps = self.pp()[:, 0:512]
            self.proj_tok(hin, sub, 512, 512, ps)
            self.cp(self.vtok[:, sub, :], ps)

    def gla_sub(self, l, d, sub, state_only):
        P = self.P
        tok = slice(sub * 128, (sub + 1) * 128)
        pb3, pb4, pb5, pb6, pb7 = self.pb[3], self.pb[4], self.pb[5], self.pb[6], self.pb[7]
        self.mm(pb3[:, 0:256], self.lraug[0:17, tok], self.wgk[0:17, l, d, :])
        self.act(self.Ektok[:, :], pb3[:, 0:256], AF.Exp, scale=-1.0)
        self.act(self.sp[:, :], self.Ektok[:, :], AF.Ln, bias=1.0)
        self.mm(pb3[:, 256:512], self.cumM[:, d, :], self.sp[:, :])
        b4 = pb4[0:64, :]
        b4v = b4.w(b4.ap.rearrange("p (h t) -> p h t", h=4))
        for h in range(4):
            self.mm(pb4[0:64, h * 128:(h + 1) * 128], self.sp[:, h * 64:(h + 1) * 64], self.cumM[:, d, :])
        self.act(self.Ektok[:, :], pb3[:, 256:512], AF.Exp, scale=-1.0)
        self.act(self.EqT[:, :, :], b4v, AF.Exp, bias=LN8)
        self.act(self.EkT[:, :, :], b4v, AF.Exp, scale=-1.0)
        slot = self.rot("Dslot", 4)
        for c in range(2):
            col = c * 64 + (63 if d == 0 else 0)
            src = b4.w(b4v.ap[:, :, col:col + 1])
            self.act(self.Dbuf[:, slot, :, c:c + 1], src, AF.Exp)
        self.tt(self.qt[:, :, :], self.qT[:, :, tok], self.EqT[:, :, :], ALU.mult)
        self.tt(self.kt[:, :, :], self.kT[:, :, tok], self.EkT[:, :, :], ALU.mult)
        self.tt(self.kttok[:, :], self.ktok[:, sub, :], self.Ektok[:, :], ALU.mult)
        if not state_only:
            p5 = pb5[:, :]
            for h in range(4):
                self.mm(pb5[:, h * 128:(h + 1) * 128], self.kt[:, h, :], self.qt[:, h, :])
            mk = self.m_mask[:, d, :]
            self.tt(self.attm[:, :, :], p5.w(p5.ap.rearrange("p (h t) -> p h t", h=4)), mk, ALU.mult,
                    bap=mk.ap.unsqueeze(1).broadcast_to([128, 4, 128]))
        first = True
        for c in ((0, 1) if d == 0 else (1, 0)):
            dprev = self.Dprev[d]
            self.tt(self.S32[:, :, :], self.Tdir[d][:, :, :], dprev, ALU.mult,
                    bap=dprev.ap.broadcast_to([64, 4, 128]))
            if not state_only:
                self.cp(self.Sbf[:, :, :], self.S32[:, :, :], eng="scalar")
                for h in range(4):
                    self.mm(pb6[:, h * 128 + c * 64:h * 128 + (c + 1) * 64], self.Sbf[:, h, :],
                            self.qt[:, h, c * 64:(c + 1) * 64], start=first, stop=False, skip=True)
                    first = False
            for h in range(4):
                self.mm(pb7[0:64, h * 128:(h + 1) * 128], self.kttok[c * 64:(c + 1) * 64, h * 64:(h + 1) * 64],
                        self.vtok[c * 64:(c + 1) * 64, sub, h * 128:(h + 1) * 128], skip=True)
            p7 = pb7[0:64, :]
            self.tt(self.Tdir[d][:, :, :], self.S32[:, :, :], p7.w(p7.ap.rearrange("p (h t) -> p h t", h=4)), ALU.add)
            self.Dprev[d] = self.Dbuf[:, slot, :, c:c + 1]
        if not state_only:
            for h in range(4):
                self.mm(pb6[:, h * 128:(h + 1) * 128], self.vtok[:, sub, h * 128:(h + 1) * 128], self.attm[:, h, :],
                        start=False, stop=True, skip=True)

    def gla_reset(self, d):
        t = self.Tdir[d][:, :, :]
        self.P.op("vector", lambda e: e.memset(t.ap, 0.0), writes=[t])
        self.Dprev[d] = self.onesc[:, :, :]

    def keep_D(self, d):
        self.cp(self.Dsave[:, d, :, :], self.Dprev[d], eng="vector")
        self.Dprev[d] = self.Dsave[:, d, :, :]

    def mixer_block_B(self, l, s, kind, t0, col, need_out):
        P, cfg = self.P, self.cfg
        src_t = self.C if kind == "ctx" else self.X
        src = src_t[s, :, :, t0:t0 + T]
        slot = self.rot("xin", 2)
        xin = self.xin[slot]
        P.cur_prio = -1
        P.dma("sync", xin[:, :, :], src.w(src.ap.rearrange("k p t -> p k t")), "xin%d" % slot)
        hin = self.hin[self.rot("hin", 2)]
        self.norm_in(xin, l, 1, col, hin)
        P.cur_prio = 0
        self.gla_proj(hin, 1)
        if need_out:
            for sub in range(2):
                ps = self.pp()[:, 0:256]
                self.proj_tok(hin, sub, 1568, 256, ps)
                self.cp(self.UT[:, t0 // 128 + sub, :], ps)
            for cc in range(2):
                pa = self.pp()[:, 0:T]
                self.proj_fm(hin, 1824 + cc * 128, 128, pa)
                pg = self.pp()[:, 0:T]
                self.proj_fm(hin, 2080 + cc * 128, 128, pg)
                sg = self.scr[self.rot("scr", 3)][:, :]
                self.act(sg, pg, AF.Sigmoid)
                self.tt(self.HB[:, cc, 16 + t0:16 + t0 + T], pa, sg, ALU.mult)
        OB = self.OBc if kind == "ctx" else self.OBl
        for sub in (1, 0):
            self.gla_sub(l, 1, sub, state_only=not need_out)
            if need_out:
                p6 = self.pb[6][:, :]
                self.cp(self.o_sb[:, :, :], p6.w(p6.ap.rearrange("p (h t) -> p h t", h=4)), eng="scalar")
                g0 = t0 + sub * 128
                P.dma("sync", OB[s, :, :, g0:g0 + 128], self.o_sb[:, :, :], "obst")

    def mixer_block_F(self, l, s, kind, t0, col, need_out, dst_t):
        P, cfg = self.P, self.cfg
        src_t = self.C if kind == "ctx" else self.X
        n_tok = cfg.nctx if kind == "ctx" else cfg.nlat
        src = src_t[s, :, :, t0:t0 + T]
        slot = self.rot("xin", 2)
        xin = self.xin[slot]
        P.cur_prio = -1
        P.dma("sync", xin[:, :, :], src.w(src.ap.rearrange("k p t -> p k t")), "xin%d" % slot)
        hin = self.hin[self.rot("hin", 2)]
        self.norm_in(xin, l, 1, col, hin)
        P.cur_prio = 0
        self.gla_proj(hin, 0)
        if not need_out:
            for sub in (0, 1):
                self.gla_sub(l, 0, sub, state_only=True)
            return
        vec = lambda j: self.vecs[:, l, j:j + 1]
        for h in range(4):
            ps = self.pp()[:, 0:T]
            self.proj_fm(hin, 1024 + h * 128, 128, ps)
            self.act(self.gs[:, h, :], ps, AF.Silu)
        for gi in range(4):
            ps = self.pp()[0:64, 0:T]
            self.proj_fm(hin, 1568 + gi * 64, 64, ps)
            self.cp(self.uT[:, gi, :], ps)
        OB = self.OBc if kind == "ctx" else self.OBl
        invd = self.invc_d if kind == "ctx" else self.invl_d
        nsub_tot = n_tok // 128
        for sub in (0, 1):
            tok = slice(sub * 128, (sub + 1) * 128)
            g0 = t0 + sub * 128
            P.dma("sync", self.ob_sb[:, :, :], OB[s, :, :, g0:g0 + 128], "obld")
            self.gla_sub(l, 0, sub, state_only=False)
            p6 = self.pb[6][:, :]
            self.tt(self.o_sb[:, :, :], p6.w(p6.ap.rearrange("p (h t) -> p h t", h=4)), self.ob_sb[:, :, :], ALU.add)
            self.act(self.attm[:, :, :], self.o_sb[:, :, :], AF.Square)
            for h in range(4):
                self.mm(self.pb[0][:, h * 128:(h + 1) * 128], self.ones_bf[:, :], self.attm[:, h, :])
            r4 = self.y_sb[:, 0:2, :]
            r4f = r4.w(r4.ap.rearrange("p a b -> p (a b)"))
            self.act(r4f, self.pb[0][:, :], AF.Ln, bias=EPS, scale=1.0 / 128)
            self.act(r4f, r4f, AF.Exp, scale=-0.5)
            self.tt(self.o_sb[:, :, :], self.o_sb[:, :, :], r4.w(r4.ap.rearrange("p a (b t) -> p (a b) t", b=2)), ALU.mult)
            self.stt(self.cat[:, 0:4, tok], self.o_sb[:, :, :], vec(0), self.gs[:, :, tok], ALU.mult, ALU.mult)
            j = g0 // 128
            P.dma("sync", self.invt[:, :, :], invd[:, :, g0:g0 + 128], "invt")
            for gi in range(4):
                dls = [dl for dl in range(-5, 6) if (kind, gi, dl) in POOLIDX and 0 <= j + dl < nsub_tot]
                for n_, dl in enumerate(dls):
                    self.mm(self.pb[7][0:64, gi * 128:(gi + 1) * 128], self.UT[:, j + dl, gi * 64:(gi + 1) * 64],
                            self.m_poolM[:, POOLIDX[(kind, gi, dl)], :], start=(n_ == 0), stop=(n_ == len(dls) - 1))
            p7 = self.pb[7][0:64, :]
            self.tt(self.invt[:, :, :], p7.w(p7.ap.rearrange("p (h t) -> p h t", h=4)), self.invt[:, :, :], ALU.mult)
            self.tt(self.dpl[:, :, :], self.invt[:, :, :], self.uT[:, :, tok], ALU.subtract)
            for cc in range(2):
                ps = self.pp()[:, 0:128]
                self.mm(ps, self.m_poolw[:, 2 * cc, :], self.dpl[:, 2 * cc, :], start=True, stop=False)
                self.mm(ps, self.m_poolw[:, 2 * cc + 1, :], self.dpl[:, 2 * cc + 1, :], start=False, stop=True)
                self.ts(self.cat[:, 4 + cc, tok], ps, vec(1 + cc), None, ALU.mult, ALU.bypass)
        NDVE = 7
        for cc in range(2):
            acc = self.cacc(cc)
            acc2 = self.y_sb[:, 4, :]
            tmpc = self.y_sb[:, 7, :]
            for jt in range(31):
                hsl = self.HB[:, cc, t0 + jt + 1:t0 + jt + 1 + T]
                wj = vec(3 + cc * 31 + jt)
                if jt == 0:
                    self.ts(acc, hsl, wj, vec(65 + cc), ALU.mult, ALU.add)
                elif cc == 0 or jt < NDVE:
                    self.stt(acc, hsl, wj, acc, ALU.mult, ALU.add)
                elif jt == NDVE:
                    self.ts(acc2, hsl, wj, 0.0, ALU.mult, ALU.add, eng="gpsimd")
                else:
                    self.ts(tmpc, hsl, wj, 0.0, ALU.mult, ALU.add, eng="gpsimd")
                    self.tt(acc2, acc2, tmpc, ALU.add, eng="gpsimd")
            if cc == 1:
                self.tt(acc, acc, acc2, ALU.add)
        st = self.pb[0]
        for cc in range(2):
            self.mm(st[:, 0:T], self.ones[:, :], self.cacc(cc), start=(cc == 0), stop=(cc == 1))
        sqs = []
        for cc in range(2):
            sq = self.scr_bf()
            self.act(sq, self.cacc(cc), AF.Square)
            sqs.append(sq)
        for cc in range(2):
            self.mm(st[:, T:2 * T], self.ones_bf[:, :], sqs[cc], start=(cc == 0), stop=(cc == 1))
        mean = self.scr[self.rot("scr", 3)][:, :]
        msq = self.scr[self.rot("scr", 3)][:, :]
        self.act(mean, st[:, 0:T], AF.Copy, scale=1.0 / 256)
        self.act(msq, st[:, 0:T], AF.Square, scale=1.0 / 256)
        rs = self.rs[self.rot("rs", 2)][:, :]
        self.stt(rs, st[:, T:2 * T], 1.0 / 256, msq, ALU.mult, ALU.subtract)
        self.act(rs, rs, AF.Ln, bias=EPS)
        self.act(rs, rs, AF.Exp, scale=-0.5)
        for cc in range(2):
            acc = self.cacc(cc)
            self.tt(acc, acc, mean, ALU.subtract)
            self.tt(acc, acc, rs, ALU.mult)
            self.ts(acc, acc, vec(67 + cc), vec(69 + cc), ALU.mult, ALU.add, eng="gpsimd")
            self.act(self.hs[:, cc, :], acc, AF.Silu)
        for co in range(2):
            ps = self.pp()[:, 0:T]
            for ci in range(2):
                self.mm(ps, self.m_pw[:, ci, co * 128:(co + 1) * 128], self.hs[:, ci, :], start=(ci == 0), stop=(ci == 1))
            self.ts(self.cat[:, 6 + co, :], ps, vec(71 + co), None, ALU.add, ALU.bypass)
        for dc in range(KC):
            py = self.pp()[:, 0:T]
            for c in range(8):
                self.mm(py, self.m_wout[:, c, dc * 128:(dc + 1) * 128], self.cat[:, c, :], start=(c == 0), stop=(c == 7))
            self.evac_with_stats(py, self.y_sb[:, dc, :], dc, KC)
        self.norm_out(self.y_sb, xin, l, 1, col)
        dst = dst_t[s, :, :, t0:t0 + T]
        P.dma("sync", dst.w(dst.ap.rearrange("k p t -> p k t")), xin[:, :, :], "xout%d" % slot)

    def mixer_sublayer(self, l, need_ctx):
        cfg = self.cfg
        self.load_mixer_weights(l)
        lr = self.lraug[:, :]
        self.P.op("vector", lambda e: e.memset(lr.ap, 1.0), writes=[lr])
        for s in range(cfg.nseq):
            for (n_tok, first) in ((cfg.nctx, True), (cfg.nlat, False)):
                pass
            self.gla_reset(1)
            self.gla_reset(0)
            if need_ctx:
                self.zero_pads(cfg.nctx)
            for t0 in reversed(range(0, cfg.nctx, T)):
                self.mixer_block_B(l, s, "ctx", t0, cfg.nseq, need_ctx)
            self.keep_D(1)
            for t0 in range(0, cfg.nctx, T):
                self.mixer_block_F(l, s, "ctx", t0, cfg.nseq, need_ctx, self.C)
            self.keep_D(0)
            self.zero_pads(cfg.nlat)
            for t0 in reversed(range(0, cfg.nlat, T)):
                self.mixer_block_B(l, s, "lat", t0, s, True)
            for t0 in range(0, cfg.nlat, T):
                self.mixer_block_F(l, s, "lat", t0, s, True, self.X)

    def zero_pads(self, n_tok):
        for (a, b) in ((0, 16), (16 + n_tok, 32 + n_tok)):
            v = self.HB[:, :, a:b]
            self.P.op("vector", lambda e, v=v: e.memset(v.ap, 0.0), writes=[v])


for _n, _f in list(MixerMixin.__dict__.items()):
    if callable(_f):
        setattr(Builder, _n, _f)


def build_full(cfg, upto=None):
    B = Builder(cfg)
    B.setup()
    B.mixer_consts()
    B.adaln()
    L = cfg.depth
    for l in range(L):
        last = l == L - 1
        first = l == 0
        stop = upto is not None and upto == (l, 0)
        B.ffn_sublayer(l, 0, 0, B.xT if first else B.X, B.outT if stop else B.X, do_ctx=True,
                       ctx_src=B.ctxT if first else B.C)
        if stop:
            break
        B.mixer_sublayer(l, need_ctx=not last)
        stop = upto is not None and upto == (l, 1)
        B.ffn_sublayer(l, 2, 1, B.X, B.outT if (last or stop) else B.X, do_ctx=not last, ctx_src=B.C)
        if stop:
            break
    if SCHED:
        B.P.schedule()
    B.P.emit()
    return B


def mixer_maps(inp, cfg):
    L = cfg.depth
    m = {}
    m["w_in"] = np.ascontiguousarray(
        inp["w_in"][:L].reshape(L, KC, 128, DIN).transpose(0, 2, 1, 3).reshape(L, 128, KC * DIN))
    m["w_out"] = np.ascontiguousarray(
        inp["w_out"][:L].reshape(L, KC, 128, D).transpose(0, 2, 1, 3).reshape(L, 128, KC * D))
    m["conv_pw"] = np.ascontiguousarray(
        inp["conv_pw"][:L].reshape(L, 2, 128, 256).transpose(0, 2, 1, 3).reshape(L, 128, 512))
    pw = np.zeros((L, 64, 4, 128), np.float32)
    for gi in range(4):
        o = (gi % 2) * 64
        pw[:, :, gi, o:o + 64] = inp["pool_w"][:L, gi]
    m["pool_w"] = pw.reshape(L, 64, 512)
    m["poolM"] = np.ascontiguousarray(POOLM.transpose(1, 0, 2).reshape(128, NPM * 128))
    ti = np.arange(128)
    same = (ti[:, None] // 64) == (ti[None, :] // 64)
    mf = (same & (ti[:, None] <= ti[None, :])).astype(np.float32)
    mb = (same & (ti[:, None] >= ti[None, :])).astype(np.float32)
    m["masks"] = np.ascontiguousarray(np.stack([mf, mb], axis=1).reshape(128, 256))
    m["cumM"] = np.ascontiguousarray(np.stack([mf, mb], axis=1) * np.float32(-1.0 / 16.0))
    wg = np.zeros((L, 32, 2, 256), np.float32)
    wg[:, 0:16] = inp["w_gk2"][:L].transpose(0, 2, 1, 3)
    wg[:, 16] = inp["b_gk"][:L]
    m["wgk"] = wg
    vec = np.zeros((128, L, NV), np.float32)
    for l in range(L):
        vec[:, l, 0] = inp["gla_norm_g"][l]
        for cc in range(2):
            sl = slice(cc * 128, (cc + 1) * 128)
            vec[:, l, 1 + cc] = inp["pool_scale"][l, sl]
            vec[:, l, 3 + cc * 31:3 + (cc + 1) * 31] = inp["conv_dw"][l][:, sl].T
            vec[:, l, 65 + cc] = inp["conv_dw_b"][l, sl]
            vec[:, l, 67 + cc] = inp["conv_ln_g"][l, sl]
            vec[:, l, 69 + cc] = inp["conv_ln_b"][l, sl]
            vec[:, l, 71 + cc] = inp["conv_pw_b"][l, sl]
    m["vecs"] = vec

    def cnt(n, w):
        lo, hi = w // 2, w - 1 - w // 2
        i = np.arange(n)
        return (np.minimum(i + hi + 1, n) - np.maximum(i - lo, 0)).astype(np.float32)
    rows = cfg.nlat // 64
    il = np.zeros((64, 4, cfg.nlat), np.float32)
    ic = np.zeros((64, 4, cfg.nctx), np.float32)
    for gi, w in enumerate(POOLW):
        c2 = (cnt(rows, w)[:, None] * cnt(64, w)[None, :]).reshape(-1)
        il[:, gi, :] = (np.float32(1.0) / c2)[None, :]
        ic[:, gi, :] = (np.float32(1.0) / cnt(cfg.nctx, w))[None, :]
    m["invc_lat"] = il
    m["invc_ctx"] = ic
    return m


_CACHE = {}


def kernel(**inputs):
    inp = {k: np.asarray(v) for k, v in inputs.items()}
    ncores = 8
    Bt, N, _ = inp["x"].shape
    cfg = Cfg(nseq=Bt // ncores, nlat=N, nctx=inp["ctx"].shape[1], depth=inp["w_ada"].shape[0])
    key = (cfg.nseq, cfg.nlat, cfg.nctx, cfg.depth)
    if key not in _CACHE:
        _CACHE[key] = build_full(cfg)
    B = _CACHE[key]
    shared = shared_weight_maps(inp, cfg.depth)
    shared.update(mixer_maps(inp, cfg))
    maps = [core_maps(inp, cfg, c, shared) for c in range(ncores)]
    res = run_bass_kernel_spmd(B.nc, maps, core_ids=list(range(ncores)))
    outs = []
    for c in range(ncores):
        oT = np.asarray(res.results[c]["outT"])
        outs.append(oT.reshape(cfg.nseq, D, N).transpose(0, 2, 1))
    return np.ascontiguousarray(np.concatenate(outs, axis=0)).astype(np.float32)
```
